# Optimizing a Trainium2 kernel written in Bass

```python
import math
import jax, jax.numpy as jnp
from jax import lax
import numpy as np

D_MODEL = 1024
BATCH = 8
SEQ = 2048
DEPTH = 1

HEAD_DIM = 128
DN_HEADS = D_MODEL // HEAD_DIM
DN_WIDTH = DN_HEADS * HEAD_DIM
POOL_WINDOWS = (2, 4, 8, 16)
POOL_GROUPS = len(POOL_WINDOWS)
POOL_WIDTH = D_MODEL // 2
POOL_GROUP_DIM = POOL_WIDTH // POOL_GROUPS
MEM_LEN = 256
MEM_HEADS = 4
MEM_WIDTH = D_MODEL // 2
MEM_HEAD_DIM = MEM_WIDTH // MEM_HEADS
CONV_WIDTH = 4
CHUNK = 64
N_BRANCH = 3
EPS = 1e-6

IN_SPLITS = (POOL_WIDTH, POOL_WIDTH, DN_WIDTH, DN_WIDTH, DN_WIDTH, DN_HEADS, DN_HEADS,
             DN_WIDTH, MEM_WIDTH, MEM_WIDTH, N_BRANCH * D_MODEL)
IN_WIDTH = int(sum(IN_SPLITS))
IN_OFFSETS = [int(o) for o in np.cumsum(IN_SPLITS)[:-1]]

kernel_name = "hybrid_pool_deltanet_memxattn_gated_merge"


def rms_norm(x, w):
    xf = x.astype(jnp.float32)
    y = xf * lax.rsqrt(jnp.mean(xf * xf, axis=-1, keepdims=True) + EPS)
    return (y * w.astype(jnp.float32)).astype(x.dtype)


def l2norm(x):
    return x * lax.rsqrt(jnp.sum(x * x, axis=-1, keepdims=True) + EPS)


def pool_mixer(u, mix_w, scale):
    B, S, _ = u.shape
    uf = u.astype(jnp.float32)
    c = jnp.cumsum(uf, axis=1)
    t = jnp.arange(1, S + 1, dtype=jnp.float32)[None, :, None]
    outs = []
    for g, w in enumerate(POOL_WINDOWS):
        sl = slice(g * POOL_GROUP_DIM, (g + 1) * POOL_GROUP_DIM)
        cg = c[..., sl]
        prev = jnp.pad(cg, ((0, 0), (w, 0), (0, 0)))[:, :S]
        mean = (cg - prev) / jnp.minimum(t, float(w))
        outs.append(mean - uf[..., sl])
    p = jnp.stack(outs, axis=2).astype(u.dtype)
    m = jnp.einsum('bsgc,gcd->bsgd', p, mix_w).reshape(B, S, POOL_WIDTH)
    return m * scale


def causal_dwconv(x, w):
    K, C = w.shape
    return lax.conv_general_dilated(x, w[:, None, :].astype(x.dtype), window_strides=(1,),
                                    padding=[(K - 1, 0)],
                                    dimension_numbers=('NWC', 'WIO', 'NWC'),
                                    feature_group_count=C)


def gated_delta_rule(q, k, v, g, beta):
    B, S, H, Dk = q.shape
    Dv = v.shape[-1]
    N = S // CHUNK

    def chunks(t):
        return t.reshape(B, N, CHUNK, H, -1).transpose(0, 3, 1, 2, 4)

    q = chunks(l2norm(q) * (Dk ** -0.5))
    k = chunks(l2norm(k))
    v = chunks(v)
    g = g.reshape(B, N, CHUNK, H).transpose(0, 3, 1, 2)
    beta = beta.reshape(B, N, CHUNK, H).transpose(0, 3, 1, 2)
    gc = jnp.cumsum(g, axis=-1)

    tril = jnp.tril(jnp.ones((CHUNK, CHUNK), dtype=bool))
    stril = jnp.tril(jnp.ones((CHUNK, CHUNK), dtype=bool), k=-1)
    diff = gc[..., :, None] - gc[..., None, :]
    decay = jnp.where(tril, jnp.exp(jnp.where(tril, diff, 0.0)), 0.0)

    kb = k * beta[..., None]
    A = jnp.where(stril, jnp.einsum('bhnid,bhnjd->bhnij', kb, k) * decay, 0.0)
    eye = jnp.eye(CHUNK, dtype=A.dtype)
    T = lax.linalg.triangular_solve(eye + A, jnp.broadcast_to(eye, A.shape),
                                    left_side=True, lower=True, unit_diagonal=True)
    u = jnp.einsum('bhnij,bhnjd->bhnid', T, v * beta[..., None])
    w = jnp.einsum('bhnij,bhnjd->bhnid', T, kb * jnp.exp(gc)[..., None])
    a_qk = jnp.where(tril, jnp.einsum('bhnid,bhnjd->bhnij', q, k) * decay, 0.0)
    qg = q * jnp.exp(gc)[..., None]
    kd = k * jnp.exp(gc[..., -1:] - gc)[..., None]
    glast = jnp.exp(gc[..., -1])

    xs = tuple(jnp.moveaxis(t, 2, 0) for t in (u, w, a_qk, qg, kd, glast))

    def step(state, inp):
        u_n, w_n, a_n, qg_n, kd_n, gl_n = inp
        v_new = u_n - jnp.einsum('bhck,bhkv->bhcv', w_n, state)
        o = jnp.einsum('bhck,bhkv->bhcv', qg_n, state) + jnp.einsum('bhcj,bhjv->bhcv', a_n, v_new)
        state = state * gl_n[..., None, None] + jnp.einsum('bhck,bhcv->bhkv', kd_n, v_new)
        return state, o

    s0 = jnp.zeros((B, H, Dk, Dv), dtype=jnp.float32)
    _, o = lax.scan(step, s0, xs)
    return o.transpose(1, 0, 3, 2, 4).reshape(B, S, H, Dv)


def memory_attention(qm, mem_n, w_kv):
    B, S, _ = qm.shape
    kv = mem_n @ w_kv
    km, vm = jnp.split(kv, 2, axis=-1)
    q = qm.reshape(B, S, MEM_HEADS, MEM_HEAD_DIM)
    km = km.reshape(B, -1, MEM_HEADS, MEM_HEAD_DIM)
    vm = vm.reshape(B, -1, MEM_HEADS, MEM_HEAD_DIM)
    s = jnp.einsum('bshd,bmhd->bhsm', q, km).astype(jnp.float32) * (MEM_HEAD_DIM ** -0.5)
    p = jax.nn.softmax(s, axis=-1).astype(vm.dtype)
    return jnp.einsum('bhsm,bmhd->bshd', p, vm).reshape(B, S, MEM_WIDTH)


def hybrid_layer(x, mem, pre_norm_w, mem_norm_w, w_in, conv_w, a_log, dt_bias, dn_norm_w,
                 pool_mix_w, pool_scale, w_mem_kv, w_proj_pool, w_proj_delta, w_proj_mem,
                 w_out, post_norm_w):
    B, S, D = x.shape
    h = rms_norm(x, pre_norm_w)
    proj = h @ w_in
    (xa, za, qd, kd, vd, a_raw, b_raw, zd, qm, zm, gate_raw) = jnp.split(proj, IN_OFFSETS, axis=-1)

    ya = pool_mixer(xa, pool_mix_w, pool_scale) * jax.nn.silu(za)

    qkv = jax.nn.silu(causal_dwconv(jnp.concatenate([qd, kd, vd], axis=-1), conv_w))
    qd, kd, vd = jnp.split(qkv, 3, axis=-1)
    shp = (B, S, DN_HEADS, HEAD_DIM)
    g = -jnp.exp(a_log.astype(jnp.float32)) * jax.nn.softplus(
        a_raw.astype(jnp.float32) + dt_bias.astype(jnp.float32))
    beta = jax.nn.sigmoid(b_raw.astype(jnp.float32))
    o = gated_delta_rule(qd.reshape(shp).astype(jnp.float32), kd.reshape(shp).astype(jnp.float32),
                         vd.reshape(shp).astype(jnp.float32), g, beta)
    yb = rms_norm(o, dn_norm_w).reshape(B, S, DN_WIDTH).astype(x.dtype) * jax.nn.silu(zd)

    yc = memory_attention(qm, rms_norm(mem, mem_norm_w), w_mem_kv) * jax.nn.silu(zm)

    gates = jax.nn.sigmoid(gate_raw).reshape(B, S, N_BRANCH, D)
    y = (gates[:, :, 0] * (ya @ w_proj_pool)
         + gates[:, :, 1] * (yb @ w_proj_delta)
         + gates[:, :, 2] * (yc @ w_proj_mem))
    out = y @ w_out
    return x + rms_norm(out, post_norm_w)


def setup_inputs(seed: int = 0) -> dict:
    key = jax.random.key(seed)
    ks = jax.random.split(key, 20)
    L, D = DEPTH, D_MODEL
    f32 = jnp.float32

    def nrm(k, shape, fan_in):
        return jax.random.normal(k, shape, f32) * (fan_in ** -0.5)

    dt = jnp.exp(jax.random.uniform(ks[8], (L, DN_HEADS), f32, math.log(1e-3), math.log(1e-1)))
    return {
        "x": jax.random.normal(ks[0], (BATCH, SEQ, D), f32),
        "mem": jax.random.normal(ks[1], (BATCH, MEM_LEN, D), f32),
        "pre_norm_w": 1.0 + 0.05 * jax.random.normal(ks[2], (L, D), f32),
        "mem_norm_w": 1.0 + 0.05 * jax.random.normal(ks[3], (L, D), f32),
        "w_in": nrm(ks[4], (L, D, IN_WIDTH), D),
        "conv_w": nrm(ks[5], (L, CONV_WIDTH, 3 * DN_WIDTH), CONV_WIDTH),
        "a_log": jnp.log(jax.random.uniform(ks[6], (L, DN_HEADS), f32, 1.0, 16.0)),
        "dt_bias": dt + jnp.log(-jnp.expm1(-dt)),
        "dn_norm_w": 1.0 + 0.05 * jax.random.normal(ks[7], (L, HEAD_DIM), f32),
        "pool_mix_w": nrm(ks[9], (L, POOL_GROUPS, POOL_GROUP_DIM, POOL_GROUP_DIM), POOL_GROUP_DIM),
        "pool_scale": 1.0 + 0.1 * jax.random.normal(ks[10], (L, POOL_WIDTH), f32),
        "w_mem_kv": nrm(ks[11], (L, D, 2 * MEM_WIDTH), D),
        "w_proj_pool": nrm(ks[12], (L, POOL_WIDTH, D), POOL_WIDTH),
        "w_proj_delta": nrm(ks[13], (L, DN_WIDTH, D), DN_WIDTH),
        "w_proj_mem": nrm(ks[14], (L, MEM_WIDTH, D), MEM_WIDTH),
        "w_out": nrm(ks[15], (L, D, D), D),
        "post_norm_w": 1.0 + 0.05 * jax.random.normal(ks[16], (L, D), f32),
    }


def reference(x, mem, pre_norm_w, mem_norm_w, w_in, conv_w, a_log, dt_bias, dn_norm_w,
              pool_mix_w, pool_scale, w_mem_kv, w_proj_pool, w_proj_delta, w_proj_mem,
              w_out, post_norm_w):
    for l in range(DEPTH):
        x = hybrid_layer(x, mem, pre_norm_w[l], mem_norm_w[l], w_in[l], conv_w[l], a_log[l],
                         dt_bias[l], dn_norm_w[l], pool_mix_w[l], pool_scale[l], w_mem_kv[l],
                         w_proj_pool[l], w_proj_delta[l], w_proj_mem[l], w_out[l], post_norm_w[l])
    return x
```

```python
import math
import numpy as np
import concourse.bass as bass
import concourse.mybir as mybir
from concourse.bass_utils import run_bass_kernel_spmd

F32 = mybir.dt.float32
BF16 = mybir.dt.bfloat16
AF = mybir.ActivationFunctionType
ALU = mybir.AluOpType

T = 2048
D = 1024
NT = 16
EPS = 1e-6
DEBUG = False
ATTACH_WAIT = True
STOP = 99
SELF_SYNC = ("act", "dve", "pool", "dma")
NEG = -30000.0

O_XA, O_ZA, O_Q, O_K, O_V, O_A, O_B, O_ZD, O_QM, O_ZM, O_G = 0, 512, 1024, 2048, 3072, 4096, 4104, 4112, 5136, 5648, 6160


class Op:
    __slots__ = ("eng", "fn", "deps", "need_inc", "ticket", "sem", "is_dma")

    def __init__(self, eng, fn):
        self.eng = eng
        self.fn = fn
        self.deps = set()
        self.need_inc = False
        self.ticket = 0
        self.sem = eng
        self.is_dma = (eng == "dma")


NDMA = 6
NGDMA = 4


def _norm_key(k):
    if k.startswith("psR"):
        return "psR"
    if k.startswith("psE"):
        return "psE"
    return k


class Prog:
    ENGS = ("pe", "act", "dve", "pool", "dma")

    def __init__(self):
        self.ops = {e: [] for e in self.ENGS}
        self.last_w = {}
        self.readers = {}
        self.pending_barrier = {e: [] for e in self.ENGS}
        self.stopped = False
        self.dma_last = [None] * NDMA
        self.dma_n = 0
        self.gdma_last = [None] * NGDMA
        self.gdma_n = 0

    def mark(self, k):
        if STOP == k:
            self.stopped = True

    def add(self, eng, fn, r=(), w=(), skip_barrier=False, gdma=False):
        if self.stopped:
            return None
        op = Op(eng, fn)
        if gdma:
            op.is_dma = True
            slot = self.gdma_n % NGDMA
            self.gdma_n += 1
            op.sem = f"gdma{slot}"
            if self.gdma_last[slot] is not None:
                op.deps.add(self.gdma_last[slot])
            self.gdma_last[slot] = op
        r = [_norm_key(k) for k in r]
        w = [_norm_key(k) for k in w]
        w = list(dict.fromkeys(w + [k for k in r if k.startswith("ps")]))
        r = [k for k in r if not k.startswith("ps")]
        if eng == "dma":
            slot = self.dma_n % NDMA
            self.dma_n += 1
            op.sem = f"dma{slot}"
            if self.dma_last[slot] is not None:
                op.deps.add(self.dma_last[slot])
            self.dma_last[slot] = op
        if not skip_barrier:
            for d in self.pending_barrier[eng]:
                op.deps.add(d)
            self.pending_barrier[eng] = []
        for k in r:
            lw = self.last_w.get(k)
            if lw is not None:
                op.deps.add(lw)
        for k in w:
            lw = self.last_w.get(k)
            if lw is not None:
                op.deps.add(lw)
            for rd in self.readers.get(k, ()):
                op.deps.add(rd)
        for k in w:
            self.last_w[k] = op
            self.readers[k] = []
        for k in r:
            self.readers.setdefault(k, []).append(op)
        op.deps.discard(op)
        self.ops[eng].append(op)
        return op

    def barrier(self):
        lasts = [self.ops[e][-1] for e in self.ENGS if self.ops[e]]
        lasts += [o for o in self.dma_last if o is not None]
        lasts += [o for o in self.gdma_last if o is not None]
        for e in self.ENGS:
            self.pending_barrier[e] = list(lasts)

    @staticmethod
    def _needs_wait(d, eng_name):
        return d.is_dma or d.eng != eng_name or eng_name in SELF_SYNC

    def finalize(self):
        for e in self.ENGS:
            for op in self.ops[e]:
                for d in op.deps:
                    if self._needs_wait(d, e):
                        d.need_inc = True
        cnt = {}
        for e in self.ENGS:
            for op in self.ops[e]:
                if op.is_dma:
                    op.need_inc = True
                if op.need_inc:
                    cnt[op.sem] = cnt.get(op.sem, 0) + 1
                    op.ticket = cnt[op.sem] * (16 if op.is_dma else 1)
        self.final_counts = {s: c * 16 for s, c in cnt.items() if s.startswith("dma") or s.startswith("gdma")}

    def emit(self, eng_name, engine, sems, final_wait=False):
        waited = {}
        for op in self.ops[eng_name]:
            need = {}
            for d in op.deps:
                if self._needs_wait(d, eng_name):
                    if d.ticket > need.get(d.sem, 0):
                        need[d.sem] = d.ticket
            todo = [(ds, tk) for ds, tk in need.items() if tk > waited.get(ds, 0)]
            for ds, tk in todo:
                waited[ds] = tk
            attach = todo.pop() if (todo and ATTACH_WAIT and not op.is_dma
                                    and eng_name in ("act", "dve", "pool", "pe")) else None
            for ds, tk in todo:
                engine.wait_ge(sems[ds], tk)
            ins = op.fn(engine)
            if attach is not None:
                ins._wait_ge(sems[attach[0]], attach[1])
            if op.need_inc:
                ins.then_inc(sems[op.sem], 16 if op.is_dma else 1)
        if final_wait:
            for s, v in self.final_counts.items():
                engine.wait_ge(sems[s], v)


def build_nc():
    nc = bass.Bass("TRN2", target_bir_lowering=False)
    P = Prog()

    def din(name, shape):
        return nc.dram_tensor(name, list(shape), F32, kind="ExternalInput").ap()

    x_d = din("x", [T, D])
    mem_d = din("mem", [256, D])
    pnw_d = din("pre_norm_w", [8, 128])
    mnw_d = din("mem_norm_w", [8, 128])
    win_d = din("w_in", [D, 9232])
    cw_d = din("conv_w", [96, 128])
    alog_d = din("a_log", [8, 1])
    dtb_d = din("dt_bias", [8, 1])
    dnw_d = din("dn_norm_w", [1, 128])
    pmw_d = din("pool_mix_w", [512, 128])
    psc_d = din("pool_scale", [4, 128])
    wkv_d = din("w_mem_kv", [D, 1024])
    wpp_d = din("w_proj_pool", [512, D])
    wpd_d = din("w_proj_delta", [D, D])
    wpm_d = din("w_proj_mem", [512, D])
    wo_d = din("w_out", [D, D])
    pnw2_d = din("post_norm_w", [1, D])
    out_d = nc.dram_tensor("out", [T, D], F32, kind="ExternalOutput").ap()
    dbg_outs = []

    def sb(name, shape, dt):
        return nc.alloc_sbuf_tensor(name, list(shape), dt)

    hT = sb("hT", [128, 8, T], BF16)
    yaT = sb("yaT", [128, 4, T], BF16)
    ybT = sb("ybT", [128, 8, T], BF16)
    ycT = sb("ycT", [128, 4, T], BF16)
    ident_bf = sb("ident_bf", [128, 128], BF16)
    ident_f = sb("ident_f", [128, 128], F32)
    UT_f = sb("UT_f", [128, 128], F32)
    ones_f = sb("ones_f", [128, 128], F32)
    ones_bf = sb("ones_bf", [128, 128], BF16)
    NEGM = sb("NEGM", [128, 384], BF16)
    SEL2 = sb("SEL2", [4, 128], BF16)
    pstage = sb("pstage", [128, 128], F32)
    pvec = sb("pvec", [128, 128], F32)
    dnw_bc = sb("dnw_bc", [128, 128], F32)
    pnw2_bc = sb("pnw2_bc", [128, D], F32)
    alog_t = sb("alog_t", [8, 1], F32)
    dtb_t = sb("dtb_t", [8, 1], F32)
    negA = sb("negA", [8, 1], F32)
    VT = sb("VT", [128, 16, 12], BF16)
    invc = sb("invc", [128, 4, 16], F32)
    gT = sb("gT", [128, 128], F32)
    lnbT = sb("lnbT", [128, 128], F32)
    gcT = sb("gcT", [128, 128], F32)
    gtotT = sb("gtotT", [128, 128], F32)
    GLT = sb("GLT", [128, 128], F32)
    betaT = sb("betaT", [128, 128], F32)
    glastT = sb("glastT", [128, 128], F32)
    small = sb("small", [128, 256], F32)
    wstage = [sb(f"wstage{i}", [128, 1024], F32) for i in range(2)]
    NWB = 4
    wblk = [sb(f"wblk{i}", [128, 1024], BF16) for i in range(NWB)]
    thb = [sb(f"thb{i}", [128, 512], F32) for i in range(2)]
    ARENA_F32 = ((nc.sbuf_bytes_remaining - 512) // 32) * 8
    arena = sb("arena", [128, ARENA_F32], F32)

    class Carver:
        def __init__(self):
            self.off = 0

        def get(self, shape, dt):
            n = 1
            for s in shape[1:]:
                n *= s
            nbytes = n * (4 if dt == F32 else 2)
            nbytes = (nbytes + 31) // 32 * 32
            assert self.off + nbytes <= ARENA_F32 * 4, (self.off, nbytes)
            o4 = self.off // 4
            ap = arena[0:shape[0], o4:o4 + nbytes // 4]
            if dt == BF16:
                ap = ap.bitcast(BF16)
                ap = ap[:, 0:n]
            else:
                ap = ap[:, 0:n]
            self.off += nbytes
            if len(shape) == 3:
                ap = ap.rearrange("p (a b) -> p a b", a=shape[1])
            elif len(shape) == 4:
                ap = ap.rearrange("p (a b c) -> p a b c", a=shape[1], b=shape[2])
            return ap

    ps01 = [nc.alloc_psum_tensor(f"ps{i}", [128, 512], F32) for i in range(2)]
    psE = nc.alloc_psum_tensor("psE", [128, 512], F32)
    psK = nc.alloc_psum_tensor("psK", [128, 512], F32)
    psT = [nc.alloc_psum_tensor(f"psT{i}", [128, 512], F32) for i in range(2)]
    psTRb = nc.alloc_psum_tensor("psTRb", [128, 1024], BF16)
    psR = nc.alloc_psum_tensor("psR", [128, 512], F32)
    psR_b = psR[:, 384:512].bitcast(BF16)
    psTRf = psTRb[:, :].bitcast(F32)

    rot = {"ps01": 0, "ws": 0, "wb": 0}

    def next_ps():
        i = rot["ps01"]
        rot["ps01"] = 1 - i
        return ps01[i], f"ps{i}"

    def dma(out, in_, r, w, arena=True):
        P.add("dma", lambda e: e.dma_start(out=out, in_=in_), r, w, skip_barrier=not arena)

    def mm(out, lhsT, rhs, start, stop, r, w):
        P.add("pe", lambda e: e.matmul(out, lhsT=lhsT, rhs=rhs, start=start, stop=stop), r, w)

    def tr(out, in_, ident, r, w):
        P.add("pe", lambda e: e.transpose(out, in_, ident), r, w)

    def act(out, in_, func, r, w, bias=None, scale=None, accum_out=None):
        kw = {}
        if bias is not None:
            kw["bias"] = bias
        if scale is not None:
            kw["scale"] = scale
        if accum_out is not None:
            kw["accum_out"] = accum_out
        P.add("act", lambda e: e.activation(out=out, in_=in_, func=func, **kw), r, w)

    def tt(eng, out, in0, in1, op, r, w):
        P.add(eng, lambda e: e.tensor_tensor(out=out, in0=in0, in1=in1, op=op), r, w)

    def ts(eng, out, in0, s1, s2, op0, op1, r, w):
        if s2 is None:
            P.add(eng, lambda e: e.tensor_scalar(out=out, in0=in0, scalar1=s1, scalar2=None, op0=op0), r, w)
        else:
            P.add(eng, lambda e: e.tensor_scalar(out=out, in0=in0, scalar1=s1, scalar2=s2, op0=op0, op1=op1), r, w)

    def stt(eng, out, in0, scalar, in1, op0, op1, r, w):
        P.add(eng, lambda e: e.scalar_tensor_tensor(out=out, in0=in0, scalar=scalar, in1=in1, op0=op0, op1=op1), r, w)

    def cp(eng, out, in_, r, w):
        if eng == "act":
            P.add("act", lambda e: e.copy(out=out, in_=in_), r, w)
        else:
            P.add(eng, lambda e: e.tensor_copy(out=out, in_=in_), r, w)

    def memset(eng, ap, val, w):
        P.add(eng, lambda e: e.memset(ap, val), (), w)

    def asel(out, in_, pattern, cmp, fill, base, cm, r, w):
        P.add("pool", lambda e: e.affine_select(out=out, in_=in_, pattern=pattern, compare_op=cmp,
                                               fill=fill, base=base, channel_multiplier=cm), r, w)

    def load_w(src, nk, n, eng="pool", pool=None):
        si = rot["ws"]
        rot["ws"] = 1 - si
        if pool is None:
            bi = rot["wb"]
            rot["wb"] = (bi + 1) % NWB
            wbuf, wkey = wblk[bi], f"wblk{bi}"
        else:
            wbuf, wkey = pool["bufs"][pool["i"]]
            pool["i"] = (pool["i"] + 1) % len(pool["bufs"])
        st = wstage[si][:, 0:nk * n].rearrange("p (k n) -> p k n", k=nk)
        wb = wbuf[:, 0:nk * n].rearrange("p (k n) -> p k n", k=nk)
        dma(st, src.rearrange("(k p) n -> p k n", p=128), [], [f"wstage{si}"], arena=False)
        cp(eng, wb, st, [f"wstage{si}"], [wkey])
        return wb, wkey

    thi = {"i": 0}

    def silu2(dst, ps_ap, pk, dkey):
        i = thi["i"]
        thi["i"] = 1 - i
        act(thb[i][:, :], ps_ap, AF.Tanh, [pk], [f"thb{i}"], scale=0.5)
        stt("dve", dst, thb[i][:, :], 1.0, ps_ap, ALU.add, ALU.mult, [f"thb{i}", pk], [dkey])

    def dbg(name, ap, shape, key):
        if not DEBUG:
            return
        dt = ap.dtype
        d = nc.dram_tensor("dbg_" + name, list(shape), dt, kind="ExternalOutput").ap()
        dma(d, ap, [key], ["dbg_" + name])
        dbg_outs.append("dbg_" + name)

    memset("pool", ident_bf[:], 0.0, ["ident_bf"])
    asel(ident_bf[:], ident_bf[:], [[-1, 128]], ALU.not_equal, 1.0, 0, 1, ["ident_bf"], ["ident_bf"])
    memset("pool", ident_f[:], 0.0, ["ident_f"])
    asel(ident_f[:], ident_f[:], [[-1, 128]], ALU.not_equal, 1.0, 0, 1, ["ident_f"], ["ident_f"])
    memset("pool", UT_f[:], 1.0, ["UT_f"])
    asel(UT_f[:], UT_f[:], [[1, 128]], ALU.is_ge, 0.0, 0, -1, ["UT_f"], ["UT_f"])
    memset("pool", ones_f[:], 1.0, ["ones_f"])
    memset("pool", ones_bf[:], 1.0, ["ones_bf"])
    memset("pool", NEGM[:], 0.0, ["NEGM"])
    asel(NEGM[:, 0:128], NEGM[:, 0:128], [[-1, 128]], ALU.is_gt, NEG, 0, 1, ["NEGM"], ["NEGM"])
    asel(NEGM[:, 128:256], NEGM[:, 128:256], [[1, 128]], ALU.is_gt, NEG, 0, -1, ["NEGM"], ["NEGM"])
    asel(NEGM[:, 256:384], NEGM[:, 256:384], [[1, 128]], ALU.is_ge, NEG, 0, -1, ["NEGM"], ["NEGM"])
    memset("pool", SEL2[:], 0.0, ["SEL2"])
    memset("pool", SEL2[0:2, :], 1.0, ["SEL2"])
    memset("pool", VT[:], 1.0, ["VT"])
    memset("pool", pstage[:], 0.0, ["pstage"])
    for g, wdw in enumerate((2, 4, 8, 16)):
        memset("pool", invc[:, g, :], 1.0 / wdw, ["invc"])
        for t_ in range(wdw - 1):
            memset("pool", invc[:, g, t_:t_ + 1], 1.0 / (t_ + 1), ["invc"])

    dma(pstage[0:8, :], pnw_d, [], ["pstage"])
    dma(pstage[8:16, :], mnw_d, [], ["pstage"])
    dma(pstage[16:112, :], cw_d, [], ["pstage"])
    dma(pstage[112:116, :], psc_d, [], ["pstage"])
    dma(alog_t[:], alog_d, [], ["alog_t"])
    dma(dtb_t[:], dtb_d, [], ["dtb_t"])
    dma(dnw_bc[:], dnw_d.partition_broadcast(128), [], ["dnw_bc"])
    ts("dve", dnw_bc[:], dnw_bc[:], 0.5, None, ALU.mult, None, ["dnw_bc"], ["dnw_bc"])
    dma(pnw2_bc[:], pnw2_d.partition_broadcast(128), [], ["pnw2_bc"])
    tr(psR[:, 0:128], pstage[:], ident_f[:], ["pstage", "ident_f"], ["psR0"])
    cp("dve", pvec[:], psR[:, 0:128], ["psR0"], ["pvec"])
    act(negA[:], alog_t[:], AF.Exp, ["alog_t"], ["negA"])
    ts("dve", negA[:], negA[:], -1.0, None, ALU.mult, None, ["negA"], ["negA"])

    PV_PNW, PV_MNW, PV_CW, PV_PSC = 0, 8, 16, 112
    ts("dve", pvec[:, PV_PSC:PV_PSC + 4], pvec[:, PV_PSC:PV_PSC + 4], 0.5, None, ALU.mult, None, ["pvec"], ["pvec"])
    dbg("pvec", pvec[:, :], [128, 128], "pvec")
    P.mark(0)

    cv = Carver()
    memnT = cv.get([128, 8, 256], BF16)
    MEMN_END = cv.off
    XT = [cv.get([128, D], F32) for _ in range(2)]
    XN = [cv.get([128, D], BF16) for _ in range(2)]
    junk0 = cv.get([128, D], BF16)
    ssq0 = small[:, 0:32]
    rstd0 = small[:, 32:64]

    def norm_tile(src_ap, i, dst, dst_key, col0, pv0):
        b = i % 2
        dma(XT[b], src_ap, [], [f"XT{b}"])
        act(junk0, XT[b], AF.Square, [f"XT{b}"], ["junk0", f"ssq0_{i}"], accum_out=ssq0[:, i:i + 1])
        act(rstd0[:, i:i + 1], ssq0[:, i:i + 1], AF.Ln, [f"ssq0_{i}"], [f"rstd0_{i}"], scale=1.0 / D, bias=EPS)
        act(rstd0[:, i:i + 1], rstd0[:, i:i + 1], AF.Exp, [f"rstd0_{i}"], [f"rstd0_{i}"], scale=-0.5)
        act(XN[b], XT[b], AF.Copy, [f"XT{b}", f"rstd0_{i}"], [f"XN{b}"], scale=rstd0[:, i:i + 1])
        for c in range(8):
            tr(psTRb[:, c * 128:(c + 1) * 128], XN[b][:, c * 128:(c + 1) * 128], ident_bf[:],
               [f"XN{b}", "ident_bf"], ["psTRb"])
        tt("dve", dst[:, :, col0:col0 + 128], psTRb[:, :].rearrange("p (c n) -> p c n", c=8),
           pvec[:, pv0:pv0 + 8].unsqueeze(2).to_broadcast([128, 8, 128]), ALU.mult,
           ["psTRb", "pvec"], [dst_key])

    for i in range(NT):
        norm_tile(x_d[i * 128:(i + 1) * 128, :], i, hT, "hT", i * 128, PV_PNW)
    for i in range(2):
        norm_tile(mem_d[i * 128:(i + 1) * 128, :], NT + i, memnT, "memnT", i * 128, PV_MNW)
    dbg("hT", hT[:, :, :], [128, 8, T], "hT")
    P.mark(1)

    def inproj(wb, wkey, nk_list, blk, M, ps, pskey, rhs_src=None, rhs_key="hT"):
        src = hT if rhs_src is None else rhs_src
        n = len(nk_list)
        for ii, kc in enumerate(nk_list):
            mm(ps[0:M, 0:512], wb[:, ii, 0:M], src[:, kc, blk * 512:(blk + 1) * 512], ii == 0, ii == n - 1,
               [wkey, rhs_key], [pskey])

    P.barrier()
    cv = Carver()
    cv.off = MEMN_END
    g_fm = cv.get([8, T], F32)
    lnb_fm = cv.get([8, T], F32)
    tmpa = cv.get([8, 512], F32)
    tmpb = cv.get([8, 512], F32)
    wab, wabk = load_w(win_d[:, O_A:O_A + 16], 8, 16)
    for blk in range(4):
        bs = slice(blk * 512, (blk + 1) * 512)
        ps, pk = next_ps()
        for kc in range(8):
            mm(ps[0:8, :], wab[:, kc, 0:8], hT[:, kc, bs], kc == 0, kc == 7, [wabk, "hT"], [pk])
        act(tmpa, ps[0:8, :], AF.Exp, [pk, "dtb_t"], ["tmpa"], bias=dtb_t[:])
        ps2, pk2 = next_ps()
        for kc in range(8):
            mm(ps2[0:8, :], wab[:, kc, 8:16], hT[:, kc, bs], kc == 0, kc == 7, [wabk, "hT"], [pk2])
        act(tmpb, ps2[0:8, :], AF.Exp, [pk2], ["tmpb"], scale=-1.0)
        act(tmpa, tmpa, AF.Ln, ["tmpa"], ["tmpa"], bias=1.0)
        act(tmpb, tmpb, AF.Ln, ["tmpb"], ["tmpb"], bias=1.0)
        ts("dve", g_fm[:, bs], tmpa, negA[:], None, ALU.mult, None, ["tmpa", "negA"], ["g_fm"])
        ts("dve", lnb_fm[:, bs], tmpb, -1.0, None, ALU.mult, None, ["tmpb"], ["lnb_fm"])
    for c in range(NT):
        tr(psR[:, c * 8:(c + 1) * 8], g_fm[:, c * 128:(c + 1) * 128], ident_f[0:8, 0:8], ["g_fm", "ident_f"], ["psR0"])
    cp("dve", gT[:], psR[:, 0:128], ["psR0"], ["gT"])
    for c in range(NT):
        tr(psR[:, 128 + c * 8:128 + (c + 1) * 8], lnb_fm[:, c * 128:(c + 1) * 128], ident_f[0:8, 0:8],
           ["lnb_fm", "ident_f"], ["psR1"])
    cp("dve", lnbT[:], psR[:, 128:256], ["psR1"], ["lnbT"])
    mm(psR[:, 256:384], UT_f[:], gT[:], True, True, ["UT_f", "gT"], ["psR2"])
    cp("dve", gcT[:], psR[:, 256:384], ["psR2"], ["gcT"])
    mm(psE[:, 0:128], ones_f[:], gT[:], True, True, ["ones_f", "gT"], ["psE"])
    cp("dve", gtotT[:], psE[:, 0:128], ["psE"], ["gtotT"])
    tt("dve", GLT[:], gcT[:], lnbT[:], ALU.add, ["gcT", "lnbT"], ["GLT"])
    act(betaT[:], lnbT[:], AF.Exp, ["lnbT"], ["betaT"], bias=-math.log(2.0))
    act(glastT[:], gtotT[:], AF.Exp, ["gtotT"], ["glastT"])

    dbg("gcT", gcT[:, :], [128, 128], "gcT")
    dbg("betaT", betaT[:, :], [128, 128], "betaT")
    P.mark(2)
    P.barrier()
    cv = Carver()
    cv.off = MEMN_END
    raw = [cv.get([128, T + 8], BF16) for _ in range(2)]
    qf = cv.get([128, T], BF16)
    kf = cv.get([128, T], BF16)
    vf = cv.get([128, T], BF16)
    szd2 = [cv.get([128, T], BF16), yaT[:, 1, :]]
    qgT2 = [cv.get([128, T], BF16), yaT[:, 0, :]]
    sqb = raw[0][:, 8:8 + T]
    dg = cv.get([128, 4, 128], BF16)
    PF = cv.get([4, 16, 2, 128], BF16)
    E2 = cv.get([4, 16, 128], BF16)
    Eq = thb[0][:, :]
    UB = 8

    def u3(ap2d):
        return ap2d.rearrange("p (u n) -> p u n", u=UB)

    OPS = [
        dict(kbg=cv.get([128, UB, 128], BF16), kd=cv.get([128, UB, 128], BF16), vb=cv.get([128, UB, 128], BF16),
             TT=cv.get([128, UB, 128], BF16), nwT=cv.get([128, UB, 128], BF16), aqk=cv.get([128, UB, 128], BF16)),
        dict(kbg=u3(yaT[:, 2, 0:1024]), kd=u3(yaT[:, 2, 1024:2048]), vb=u3(yaT[:, 3, 0:1024]),
             TT=u3(yaT[:, 3, 1024:2048]), nwT=u3(ycT[:, 0, 0:1024]), aqk=u3(ycT[:, 0, 1024:2048])),
    ]
    GU = 4
    NSLOT = 8
    TS = [[cv.get([128, 384], BF16) for _ in range(2)] for _ in range(GU)]
    Dall = [cv.get([128, 384], F32)] * 2
    AB0 = [cv.get([128, 256], BF16) for _ in range(GU)]
    Xa = [cv.get([128, 128], BF16) for _ in range(GU)]
    Rb = [cv.get([128, 128], BF16) for _ in range(GU)]
    XTn = [cv.get([128, 128], BF16) for _ in range(GU)]
    IA0 = [cv.get([128, 128], BF16) for _ in range(GU)]
    ycx = ycT[:, 1:4, :].rearrange("p a b -> p (a b)")
    for sl_ in range(4):
        o_ = sl_ * 1536
        AB0.append(ycx[:, o_:o_ + 256])
        TS.append([ycx[:, o_ + 256:o_ + 640], ycx[:, o_ + 640:o_ + 1024]])
        IA0.append(ycx[:, o_ + 1024:o_ + 1152])
        Xa.append(ycx[:, o_ + 1152:o_ + 1280])
        Rb.append(ycx[:, o_ + 1280:o_ + 1408])
        XTn.append(ycx[:, o_ + 1408:o_ + 1536])
    otok = cv.get([128, 16, 128], BF16)
    onb = [cv.get([128, 128], BF16) for _ in range(2)]
    S_f = cv.get([128, 128], F32)
    S_b = cv.get([128, 128], BF16)
    vnew_b = cv.get([128, 128], BF16)
    X3 = cv.get([128, 16, 3], F32)
    Lq = cv.get([128, 16], F32)
    Lk = cv.get([128, 16], F32)
    sc_kbg = cv.get([128, 16], F32)
    sc_kd = cv.get([128, 16], F32)
    tmp16 = cv.get([128, 16], F32)
    ossq = cv.get([128, 16], F32)
    orstd = cv.get([128, 16], F32)
    junkD = cv.get([128, 128], BF16)
    print("delta arena bytes used", cv.off, "of", ARENA_F32 * 4)

    for rb in range(2):
        memset("pool", raw[rb][:, 0:3], 0.0, [f"raw{rb}"])

    gcT3 = gcT[:, :].rearrange("p (c h) -> p c h", h=8)
    GLT3 = GLT[:, :].rearrange("p (c h) -> p c h", h=8)
    gtotT3 = gtotT[:, :].rearrange("p (c h) -> p c h", h=8)
    betaT3 = betaT[:, :].rearrange("p (c h) -> p c h", h=8)

    ev = {"i": 0}

    def evac(out, in_, r, w):
        ev["i"] ^= 1
        cp("act" if ev["i"] else "dve", out, in_, r, w)

    def s1x(h, which):
        p = h % 2
        lst = ((0, O_Q, qf, "qf", 0), (1, O_K, kf, "kf", 1)) if which == "qk" else ((2, O_V, vf, "vf", 1),)
        for t_i, c0, dst, dkey, ri in lst:
            wb, wk = load_w(win_d[:, c0 + h * 128:c0 + (h + 1) * 128], 8, 128)
            rw = raw[ri]
            rk = f"raw{ri}"
            for blk in range(4):
                ps, pk = next_ps()
                inproj(wb, wk, range(8), blk, 128, ps, pk)
                cp("act", rw[:, 3 + blk * 512:3 + (blk + 1) * 512], ps[:, :], [pk], [rk])
                yield
            ch = t_i * 8 + h
            for j in range(4):
                col = PV_CW + j * 24 + ch
                ts("dve", dg[:, j, :], ident_bf[:], pvec[:, col:col + 1], None, ALU.mult, None,
                   ["ident_bf", "pvec"], ["dg"])
            for blk in range(4):
                ps, pk = next_ps()
                for j in range(4):
                    mm(ps[:, :], dg[:, j, :], rw[:, j + blk * 512:j + blk * 512 + 512], j == 0, j == 3,
                       ["dg", rk], [pk])
                silu2(dst[:, blk * 512:(blk + 1) * 512], ps[:, :], pk, dkey)
                yield
        if which == "vz":
            wb, wk = load_w(win_d[:, O_ZD + h * 128:O_ZD + (h + 1) * 128], 8, 128)
            for blk in range(4):
                ps, pk = next_ps()
                inproj(wb, wk, range(8), blk, 128, ps, pk)
                silu2(szd2[p][:, blk * 512:(blk + 1) * 512], ps[:, :], pk, f"szd{p}")
                yield
    N_S1QK = 16
    N_S1VZ = 12
    N_S1 = 28

    def s2(h):
        p = h % 2
        act(sqb, qf, AF.Square, ["qf"], ["raw0"])
        yield
        for c in range(NT):
            mm(psR[:, c:c + 1], sqb[:, c * 128:(c + 1) * 128], ones_bf[:, 0:1], True, True, ["raw0", "ones_bf"], ["psR0"])
        act(Lq, psR[:, 0:16], AF.Ln, ["psR0"], ["Lq"], scale=128.0, bias=128.0 * 4.0 * EPS)
        yield
        act(sqb, kf, AF.Square, ["kf"], ["raw0"])
        yield
        for c in range(NT):
            mm(psR[:, 128 + c:128 + c + 1], sqb[:, c * 128:(c + 1) * 128], ones_bf[:, 0:1], True, True,
               ["raw0", "ones_bf"], ["psR1"])
        act(Lk, psR[:, 128:144], AF.Ln, ["psR1"], ["Lk"], bias=4.0 * EPS)
        yield
        stt("dve", X3[:, :, 0], Lk, -0.5, GLT3[:, :, h], ALU.mult, ALU.add, ["Lk", "GLT"], ["X3"])
        stt("dve", X3[:, :, 1], Lq, -0.5, gcT3[:, :, h], ALU.mult, ALU.add, ["Lq", "gcT"], ["X3"])
        stt("dve", X3[:, :, 2], Lk, -0.5, gcT3[:, :, h], ALU.mult, ALU.subtract, ["Lk", "gcT"], ["X3"])
        act(sc_kbg, X3[:, :, 0], AF.Exp, ["X3"], ["sc_kbg"])
        tt("dve", tmp16, X3[:, :, 2], gtotT3[:, :, h], ALU.add, ["X3", "gtotT"], ["tmp16"])
        act(sc_kd, tmp16, AF.Exp, ["tmp16"], ["sc_kd"])
        cp("dve", VT[:, :, 0:8:4], X3[:, :, 0:2], ["X3"], ["VT"])
        cp("dve", VT[:, :, 10:11], X3[:, :, 2:3], ["X3"], ["VT"])
        tt("dve", VT[:, :, 1:9:4], X3[:, :, 0:2], VT[:, :, 0:8:4], ALU.subtract, ["X3", "VT"], ["VT"])
        tt("dve", VT[:, :, 11:12], X3[:, :, 2:3], VT[:, :, 10:11], ALU.subtract, ["X3", "VT"], ["VT"])
        yield
        for gq in range(4):
            for cc in range(4):
                c = gq * 4 + cc
                for v in range(2):
                    o = (cc * 2 + v) * 128
                    tr(psTRb[0:4, o:o + 128], VT[:, c, v * 4:v * 4 + 4], ident_bf[:], ["VT", "ident_bf"], ["psTRb"])
            evac(PF[:, gq * 4:(gq + 1) * 4, :, :], psTRb[0:4, :].rearrange("p (c v n) -> p c v n", c=4, v=2),
                 ["psTRb"], ["PF"])
            yield
        for g8 in range(2):
            for cc in range(8):
                c = g8 * 8 + cc
                tr(psTRb[0:4, cc * 128:(cc + 1) * 128], VT[:, c, 8:12], ident_bf[:], ["VT", "ident_bf"], ["psTRb"])
            evac(E2[:, g8 * 8:(g8 + 1) * 8, :], psTRb[0:4, :].rearrange("p (c n) -> p c n", c=8), ["psTRb"], ["E2"])
            yield
        for blk in range(4):
            ps, pk = next_ps()
            mm(ps[:, :], SEL2[:, :], PF[:, blk * 4:(blk + 1) * 4, 1, :], True, True, ["SEL2", "PF"], [pk])
            act(Eq, ps[:, :], AF.Exp, [pk], ["thb0"])
            tt("dve", qgT2[p][:, blk * 512:(blk + 1) * 512], qf[:, blk * 512:(blk + 1) * 512], Eq, ALU.mult,
               ["qf", "thb0"], [f"qgT{p}"])
            yield
    N_S2 = 15

    TB = [(psT[0], "psT0"), (psT[1], "psT1"), (ps01[0], "ps0"), (ps01[1], "ps1")]
    NST = 6

    def S3(h):
        def slot_of(c):
            return c % NSLOT

        def setup_unit(c):
            ub = c // UB
            O = OPS[ub]
            so = ub
            lc = c - ub * UB
            u = slot_of(c)
            cs = slice(c * 128, (c + 1) * 128)
            tr(psR_b[:, 0:128], kf[:, cs], ident_bf[:], ["kf", "ident_bf"], ["psR3"])
            tr(psR_b[:, 128:256], vf[:, cs], ident_bf[:], ["vf", "ident_bf"], ["psR3"])
            P.add("act", (lambda o_, i_, s_: (lambda e: e.activation(out=o_, in_=i_, func=AF.Copy, scale=s_)))(
                O["kbg"][:, lc, :], psR_b[:, 0:128], sc_kbg[:, c:c + 1]), ["psR3", "sc_kbg"], [f"kbg{so}_{lc}"])
            ts("dve", O["kd"][:, lc, :], psR_b[:, 0:128], sc_kd[:, c:c + 1], None, ALU.mult, None,
               ["psR3", "sc_kd"], [f"kd{so}_{lc}"])
            ts("dve", O["vb"][:, lc, :], psR_b[:, 128:256], betaT3[:, c, h:h + 1], None, ALU.mult, None,
               ["psR3", "betaT"], [f"vb{so}_{lc}"])
            mm(psE[:, 0:384], ident_bf[:], NEGM[:, 0:384], True, False, ["ident_bf", "NEGM"], ["psE"])
            mm(psE[:, 0:128], PF[:, c, 0, :], E2[:, c, :], False, False, ["PF", "E2"], ["psE"])
            mm(psE[:, 128:384], E2[:, c, :], PF[:, c, :, :], False, True, ["PF", "E2"], ["psE"])
            Dl = Dall[c % 2]
            dk_ = "Dall"
            act(Dl, psE[:, 0:384], AF.Exp, ["psE"], [dk_])
            mm(psK[:, 0:128], kf[:, cs], kf[:, cs], True, True, ["kf"], ["psK"])
            mm(psK[:, 128:256], kf[:, cs], kf[:, cs], True, True, ["kf"], ["psK"])
            mm(psK[:, 256:384], kf[:, cs], qf[:, cs], True, True, ["kf", "qf"], ["psK"])
            tsb = TS[u][1]
            tt("dve", AB0[u][:, 0:256], psK[:, 0:256], Dl[:, 0:256], ALU.mult, ["psK", dk_], [f"AB0{u}"])
            tt("dve", O["aqk"][:, lc, :], psK[:, 256:384], Dl[:, 256:384], ALU.mult, ["psK", dk_], [f"aqk{so}_{lc}"])
            tt("pool", tsb[:, 256:384], ident_bf[:], AB0[u][:, 128:256], ALU.subtract,
               ["ident_bf", f"AB0{u}"], [f"TS{u}b"])
            tt("pool", IA0[u], ident_bf[:], AB0[u][:, 0:128], ALU.add, ["ident_bf", f"AB0{u}"], [f"IA0{u}"])

        def iter_step(g, s):
            for i_ in range(GU):
                c = g * GU + i_
                u = slot_of(c)
                cur, nxt = (TS[u][0], TS[u][1]) if s % 2 == 0 else (TS[u][1], TS[u][0])
                ck = f"TS{u}a" if s % 2 == 0 else f"TS{u}b"
                if s == 0:
                    cur, ck = AB0[u], f"AB0{u}"
                nk_ = f"TS{u}b" if s % 2 == 0 else f"TS{u}a"
                pt, ptk = TB[i_]
                if s == 0:
                    mm(pt[:, 0:128], cur[:, 128:256], cur[:, 0:128], True, True, [ck], [ptk])
                    mm(pt[:, 128:256], cur[:, 0:128], cur[:, 128:256], True, True, [ck], [ptk])
                elif s < NST - 1:
                    mm(pt[:, 0:128], cur[:, 128:256], cur[:, 0:128], True, True, [ck], [ptk])
                    mm(pt[:, 128:384], cur[:, 0:128], cur[:, 128:384], True, False, [ck], [ptk])
                    mm(pt[:, 256:384], ident_bf[:], cur[:, 256:384], False, True, [ck, "ident_bf"], [ptk])
                else:
                    mm(pt[:, 256:384], cur[:, 0:128], cur[:, 256:384], True, False, [ck], [ptk])
                    mm(pt[:, 256:384], ident_bf[:], cur[:, 256:384], False, True, [ck, "ident_bf"], [ptk])
                if s == 0:
                    evac(nxt[:, 0:256], pt[:, 0:256], [ptk], [nk_])
                elif s < NST - 1:
                    evac(nxt[:, 0:384], pt[:, 0:384], [ptk], [nk_])
                else:
                    evac(Xa[u], pt[:, 256:384], [ptk], [f"Xa{u}"])

        def fin_parts(c):
            ub = c // UB
            O = OPS[ub]
            so = ub
            lc = c - ub * UB
            u = slot_of(c)

            def p1():
                mm(psTRf[:, 256:384], IA0[u], Xa[u], True, True, [f"IA0{u}", f"Xa{u}"], ["psTRb"])
                tr(psTRb[:, 0:128], Xa[u], ident_bf[:], [f"Xa{u}", "ident_bf"], ["psTRb"])
                stt("dve", Rb[u], ident_bf[:], 2.0, psTRf[:, 256:384], ALU.mult, ALU.subtract,
                    ["ident_bf", "psTRb"], [f"Rb{u}"])
                cp("act", XTn[u], psTRb[:, 0:128], ["psTRb"], [f"XTn{u}"])

            def p2():
                mm(psTRf[:, 384:512], XTn[u], Rb[u], True, True, [f"XTn{u}", f"Rb{u}"], ["psTRb"])
                evac(O["TT"][:, lc, :], psTRf[:, 384:512], ["psTRb"], [f"TT{so}_{lc}"])

            def p3():
                mm(psE[:, 384:512], O["kbg"][:, lc, :], O["TT"][:, lc, :], True, True,
                   [f"kbg{so}_{lc}", f"TT{so}_{lc}"], ["psEw"])
                ts("dve", O["nwT"][:, lc, :], psE[:, 384:512], -1.0, None, ALU.mult, None, ["psEw"], [f"nwT{so}_{lc}"])
            return [p1, p2, p3]

        NG = NT // GU
        for i_ in range(GU):
            setup_unit(i_)
            yield
        for g in range(NG):
            fins = [fin_parts(c) for c in range((g - 1) * GU, g * GU)] if g > 0 else None
            for s in range(NST):
                iter_step(g, s)
                yield
                if fins is not None:
                    if 1 <= s <= GU:
                        fins[s - 1][1]()
                    if 2 <= s <= GU + 1:
                        fins[s - 2][2]()
                    if s < GU:
                        fins[s][0]()
                    yield
                if g + 1 < NG and 2 <= s <= GU + 1:
                    setup_unit((g + 1) * GU + s - 2)
                    yield
        for c in range((NG - 1) * GU, NG * GU):
            for f_ in fin_parts(c):
                f_()
                yield
    N_S3H = 4 + 4 * 6 + 3 * 4 + 4 * 12

    def s4(h, ub, so):
        p = h % 2
        O = OPS[so]
        for lc in range(UB):
            c = ub * UB + lc
            cs = slice(c * 128, (c + 1) * 128)
            first = (c == 0)
            mm(psR[:, 0:128], O["TT"][:, lc, :], O["vb"][:, lc, :], True, first,
               [f"TT{so}_{lc}", f"vb{so}_{lc}"], ["psR0"])
            if not first:
                mm(psR[:, 0:128], O["nwT"][:, lc, :], S_b, False, True, [f"nwT{so}_{lc}", "S_b"], ["psR0"])
            cp("act", vnew_b, psR[:, 0:128], ["psR0"], ["vnew_b"])
            yield
            mm(psE[:, 0:128], O["kd"][:, lc, :], vnew_b, True, True, [f"kd{so}_{lc}", "vnew_b"], ["psE"])
            if not first:
                mm(psK[:, 0:128], qgT2[p][:, cs], S_b, True, False, [f"qgT{p}", "S_b"], ["psK"])
            mm(psK[:, 0:128], O["aqk"][:, lc, :], vnew_b, first, True, [f"aqk{so}_{lc}", "vnew_b"], ["psK"])
            if first:
                cp("dve", S_b, psE[:, 0:128], ["psE"], ["S_b"])
                cp("dve", S_f, psE[:, 0:128], ["psE"], ["S_f"])
            else:
                gl = glastT[:, c * 8 + h:c * 8 + h + 1]
                stt("dve", S_b, S_f, gl, psE[:, 0:128], ALU.mult, ALU.add, ["S_f", "glastT", "psE"], ["S_b"])
                stt("dve", S_f, S_f, gl, psE[:, 0:128], ALU.mult, ALU.add, ["S_f", "glastT", "psE"], ["S_f"])
            act(junkD, psK[:, 0:128], AF.Square, ["psK"], ["junkD", "ossq"], accum_out=ossq[:, c:c + 1])
            cp("dve", otok[:, c, :], psK[:, 0:128], ["psK"], ["otok"])
            yield
    N_S4 = 16

    def s5(h):
        p = h % 2
        act(orstd, ossq, AF.Ln, ["ossq"], ["orstd"], scale=1.0 / 128, bias=EPS)
        act(orstd, orstd, AF.Exp, ["orstd"], ["orstd"], scale=-0.5)
        for c in range(NT):
            stt("dve", otok[:, c, :], otok[:, c, :], orstd[:, c:c + 1], dnw_bc[:], ALU.mult, ALU.mult,
                ["otok", "orstd", "dnw_bc"], ["otok"])
            if c % 4 == 3:
                yield
        for blk in range(4):
            for cc in range(4):
                c = blk * 4 + cc
                tr(psTRb[:, cc * 128:(cc + 1) * 128], otok[:, c, :], ident_bf[:], ["otok", "ident_bf"], ["psTRb"])
            tt("dve", ybT[:, h, blk * 512:(blk + 1) * 512], psTRb[:, 0:512], szd2[p][:, blk * 512:(blk + 1) * 512],
               ALU.mult, ["psTRb", f"szd{p}"], ["ybT"])
            yield
    N_S5 = 8

    def run(g):
        for _ in g:
            pass

    def chain(*gs):
        for g in gs:
            yield from g

    def par(ga, na, gb, nb):
        da = db = 0
        alive_a = alive_b = True
        while alive_a or alive_b:
            pick_a = alive_a and (not alive_b or da * nb <= db * na)
            if pick_a:
                try:
                    next(ga)
                    da += 1
                except StopIteration:
                    alive_a = False
            else:
                try:
                    next(gb)
                    db += 1
                except StopIteration:
                    alive_b = False

    def par(*pairs):
        gens = [[pairs[i], pairs[i + 1], 0, True] for i in range(0, len(pairs), 2)]
        while any(g[3] for g in gens):
            best = None
            for g in gens:
                if g[3] and (best is None or g[2] * best[1] < best[2] * g[1]):
                    best = g
            try:
                next(best[0])
                best[2] += 1
            except StopIteration:
                best[3] = False

    def gpar(*pairs):
        gens = [[pairs[i], pairs[i + 1], 0, True] for i in range(0, len(pairs), 2)]
        while any(g[3] for g in gens):
            best = None
            for g in gens:
                if g[3] and (best is None or g[2] * best[1] < best[2] * g[1]):
                    best = g
            try:
                next(best[0])
                best[2] += 1
                yield
            except StopIteration:
                best[3] = False

    run(s1x(0, "qk"))
    run(s1x(0, "vz"))
    run(s2(0))
    for h in range(8):
        if h == 0:
            run(S3(0))
        elif h < 7:
            par(S3(h), N_S3H, chain(s4(h - 1, 1, 1), s5(h - 1)), 2 * (N_S4 + N_S5))
        else:
            par(S3(h), N_S3H, chain(s4(h - 1, 1, 1), s5(h - 1), s4(h, 0, 0)), 2 * (N_S4 + N_S5))
        if h < 7:
            par(chain(s1x(h + 1, "qk"), gpar(s1x(h + 1, "vz"), N_S1VZ, s2(h + 1), N_S2)), N_S1QK + N_S1VZ + N_S2,
                s4(h, 0, 0), N_S4)
    run(s4(7, 1, 1))
    run(s5(7))
    dbg("ybT", ybT[:, :, :], [128, 8, T], "ybT")
    P.mark(3)

    P.barrier()
    cv = Carver()
    cv.off = MEMN_END
    A_ = dict(U=cv.get([128, 16 + T], F32), S1=cv.get([128, 16 + T], F32), S2=cv.get([128, 16 + T], F32),
              sza=cv.get([128, T], BF16), pTb=cv.get([128, T], BF16), ptmp=cv.get([128, 16], F32))
    for nm in ("U", "S1", "S2"):
        memset("pool", A_[nm][:, 0:16], 0.0, [f"{nm}A"])
    poolA = {"bufs": [(cv.get([128, 1024], BF16), f"wA{i}") for i in range(3)], "i": 0}
    poolC = {"bufs": [(cv.get([128, 1024], BF16), f"wC{i}") for i in range(4)], "i": 0}
    kmT = cv.get([128, 4, 256], BF16)
    vm = cv.get([128, 2, 512], BF16)
    qmT2 = [cv.get([128, 512], BF16) for _ in range(2)]
    szm2 = [cv.get([128, 512], BF16) for _ in range(2)]
    pT4 = [[cv.get([128, 512], BF16) for _ in range(2)] for _ in range(2)]
    rden2 = [cv.get([128, 512], F32) for _ in range(2)]
    tnum2 = [cv.get([128, 512], F32) for _ in range(2)]
    print("A+C arena bytes used", cv.off, "of", ARENA_F32 * 4)

    def phaseA():
        U, S1, S2, sza, pTb, ptmp = A_["U"], A_["S1"], A_["S2"], A_["sza"], A_["pTb"], A_["ptmp"]
        kU, kS1, kS2, ksza, kpT, kpt = ("UA", "S1A", "S2A", "szaA", "pTbA", "ptmpA")
        for g, wdw in enumerate((2, 4, 8, 16)):
            wb, wk = load_w(win_d[:, O_XA + g * 128:O_XA + (g + 1) * 128], 8, 128, pool=poolA, eng="act")
            for blk in range(4):
                ps, pk = next_ps()
                inproj(wb, wk, range(8), blk, 128, ps, pk)
                cp("act", U[:, 16 + blk * 512:16 + (blk + 1) * 512], ps[:, :], [pk], [kU])
                yield
            wb, wk = load_w(win_d[:, O_ZA + g * 128:O_ZA + (g + 1) * 128], 8, 128, pool=poolA, eng="act")
            for blk in range(4):
                ps, pk = next_ps()
                inproj(wb, wk, range(8), blk, 128, ps, pk)
                silu2(sza[:, blk * 512:(blk + 1) * 512], ps[:, :], pk, ksza)
                yield
            wm, wmk = load_w(pmw_d[g * 128:(g + 1) * 128, :], 1, 128, pool=poolA, eng="act")
            src_, sk = U, kU
            sh = 1
            bufs = [(S1, kS1), (S2, kS2)]
            bi = 0
            while sh < wdw:
                dst, dk2 = bufs[bi]
                bi ^= 1
                tt("dve", dst[:, 16:16 + T], src_[:, 16:16 + T], src_[:, 16 - sh:16 - sh + T], ALU.add, [sk], [dk2])
                src_, sk = dst, dk2
                sh *= 2
                yield
            stt("dve", pTb[:, 16:T], src_[:, 32:16 + T], 1.0 / wdw, U[:, 32:16 + T], ALU.mult, ALU.subtract,
                [sk, kU], [kpT])
            tt("dve", ptmp, src_[:, 16:32], invc[:, g, :], ALU.mult, [sk, "invc"], [kpt])
            tt("dve", pTb[:, 0:16], ptmp, U[:, 16:32], ALU.subtract, [kpt, kU], [kpT])
            yield
            for blk in range(4):
                ps, pk = next_ps()
                mm(ps[:, :], wm[:, 0, :], pTb[:, blk * 512:(blk + 1) * 512], True, True, [wmk, kpT], [pk])
                stt("dve", yaT[:, g, blk * 512:(blk + 1) * 512], ps[:, :], pvec[:, PV_PSC + g:PV_PSC + g + 1],
                    sza[:, blk * 512:(blk + 1) * 512], ALU.mult, ALU.mult, [pk, "pvec", ksza], ["yaT"])
                yield
    N_A = 4 * (4 + 4 + 3 + 1 + 4)

    def phaseC():
        ci_ = 0
        for hh in range(4):
            wb, wk = load_w(wkv_d[:, hh * 128:(hh + 1) * 128], 8, 128, pool=poolC, eng="act")
            ps, pk = next_ps()
            for kc in range(8):
                mm(ps[:, 0:256], wb[:, kc, :], memnT[:, kc, :], kc == 0, kc == 7, [wk, "memnT"], [pk])
            evac(kmT[:, hh, :], ps[:, 0:256], [pk], ["kmT"])
            yield
        for hh in range(4):
            wb, wk = load_w(wkv_d[:, 512 + hh * 128:512 + (hh + 1) * 128], 8, 128, pool=poolC, eng="act")
            for mt in range(2):
                ps, pk = next_ps()
                for kc in range(8):
                    mm(ps[:, 0:128], memnT[:, kc, mt * 128:(mt + 1) * 128], wb[:, kc, :], kc == 0, kc == 7,
                       [wk, "memnT"], [pk])
                act(vm[:, mt, hh * 128:(hh + 1) * 128], ps[:, 0:128], AF.Copy, [pk], ["vm"], scale=0.5)
            yield
        for hh in range(4):
            wq, wqk = load_w(win_d[:, O_QM + hh * 128:O_QM + (hh + 1) * 128], 8, 128, pool=poolC, eng="act")
            wz, wzk = load_w(win_d[:, O_ZM + hh * 128:O_ZM + (hh + 1) * 128], 8, 128, pool=poolC, eng="act")
            for blk in range(4):
                bs = slice(blk * 512, (blk + 1) * 512)
                q_ = ci_ % 2
                ci_ += 1
                qmT, szm, pT, rden, tnum = qmT2[q_], szm2[q_], pT4[q_], rden2[q_], tnum2[q_]
                pso_, pso_k = (psE, "psE") if q_ == 0 else (psR, "psR0")
                psd_, psd_k = (psK, "psK") if q_ == 0 else (psTRf, "psTRb")
                ps, pk = next_ps()
                inproj(wq, wqk, range(8), blk, 128, ps, pk)
                act(qmT, ps[:, :], AF.Copy, [pk], [f"qmT{q_}"], scale=128.0 ** -0.5)
                ps, pk = next_ps()
                inproj(wz, wzk, range(8), blk, 128, ps, pk)
                silu2(szm, ps[:, :], pk, f"szm{q_}")
                yield
                for mt in range(2):
                    pst = psT[mt]
                    mm(pst[:, :], kmT[:, hh, mt * 128:(mt + 1) * 128], qmT, True, True, ["kmT", f"qmT{q_}"], [f"psT{mt}"])
                    act(pT[mt], pst[:, :], AF.Exp, [f"psT{mt}"], [f"pT{q_}{mt}"])
                for mt in range(2):
                    mm(pso_[:, :], vm[:, mt, hh * 128:(hh + 1) * 128], pT[mt], mt == 0, mt == 1,
                       ["vm", f"pT{q_}{mt}"], [pso_k])
                for mt in range(2):
                    mm(psd_[:, :], ones_bf[:, :], pT[mt], mt == 0, mt == 1, ["ones_bf", f"pT{q_}{mt}"], [psd_k])
                yield
                P.add("dve", (lambda o_, i_: (lambda e: e.reciprocal(out=o_, in_=i_)))(rden, psd_[:, :]), [psd_k], [f"rden{q_}"])
                tt("dve", tnum, pso_[:, :], rden, ALU.mult, [pso_k, f"rden{q_}"], [f"tnum{q_}"])
                tt("dve", ycT[:, hh, bs], tnum, szm, ALU.mult, [f"tnum{q_}", f"szm{q_}"], ["ycT"])
                yield
    N_C = 8 + 16 * 3

    par(phaseA(), N_A, phaseC(), N_C)
    dbg("yaT", yaT[:, :, :], [128, 4, T], "yaT")
    P.mark(4)
    dbg("ycT", ycT[:, :, :], [128, 4, T], "ycT")
    P.mark(5)

    P.barrier()
    cv = Carver()
    cv.off = 0
    yT = cv.get([128, 8, T], BF16)
    sg = [cv.get([128, 512], F32) for _ in range(2)]
    acc = cv.get([128, T], F32)
    wo = cv.get([128, 8, D], BF16)
    XT2 = [cv.get([128, D], F32) for _ in range(4)]
    junkO = cv.get([128, 512], BF16)
    tmo = [thb[0][:, :], thb[1][:, :]]
    tmpm = tmo[0]
    branches = ((yaT, "yaT", 4, wpp_d), (ybT, "ybT", 8, wpd_d), (ycT, "ycT", 4, wpm_d))
    for cb in range(8):
        P.add("pool", (lambda o_, i_: (lambda e: e.dma_start(out=o_, in_=i_)))(
            wo[:, :, cb * 128:(cb + 1) * 128], wo_d[:, cb * 128:(cb + 1) * 128].rearrange("(k p) n -> p k n", p=128)),
            [], ["wo"], gdma=True)
    gi_ = 0
    for oc in range(8):
        for br in range(3):
            ysrc, ykey, nk, wd = branches[br]
            wg, wgk = load_w(win_d[:, O_G + br * 1024 + oc * 128:O_G + br * 1024 + (oc + 1) * 128], 8, 128, eng="act")
            wp, wpk = load_w(wd[:, oc * 128:(oc + 1) * 128], nk, 128, eng="pool")
            for blk in range(4):
                bs = slice(blk * 512, (blk + 1) * 512)
                ps, pk = next_ps()
                inproj(wg, wgk, range(8), blk, 128, ps, pk)
                s_ = sg[gi_ % 2]
                sk_ = f"sg{gi_ % 2}"
                gi_ += 1
                act(s_, ps[:, :], AF.Sigmoid, [pk], [sk_])
                ps2, pk2 = next_ps()
                inproj(wp, wpk, range(nk), blk, 128, ps2, pk2, rhs_src=ysrc, rhs_key=ykey)
                if br == 0:
                    tt("dve", acc[:, bs], ps2[:, :], s_, ALU.mult, [pk2, sk_], ["acc"])
                elif br == 1:
                    tt("dve", tmpm, ps2[:, :], s_, ALU.mult, [pk2, sk_], ["thb0"])
                    tt("dve", acc[:, bs], acc[:, bs], tmpm, ALU.add, ["acc", "thb0"], ["acc"])
                else:
                    tt("dve", tmpm, ps2[:, :], s_, ALU.mult, [pk2, sk_], ["thb0"])
                    tt("dve", yT[:, oc, bs], acc[:, bs], tmpm, ALU.add, ["acc", "thb0"], ["yT"])
    dbg("yT", yT[:, :, :], [128, 8, T], "yT")
    P.mark(6)

    ssqo = small[:, 64:96]
    rso = small[:, 96:112]
    PRE = 3
    for i in range(PRE):
        dma(XT2[i % 4], x_d[i * 128:(i + 1) * 128, :], [], [f"XT2{i % 4}"])
    for i in range(NT):
        b = i % 4
        if i + PRE < NT:
            dma(XT2[(i + PRE) % 4], x_d[(i + PRE) * 128:(i + PRE + 1) * 128, :], [], [f"XT2{(i + PRE) % 4}"])
        pso = [[psT[0], psT[1]], [psE, psK], [ps01[0], ps01[1]], [psR, psTRf]][b]
        psok = [["psT0", "psT1"], ["psE", "psK"], ["ps0", "ps1"], ["psR0", "psTRb"]][b]
        for half in range(2):
            for kc in range(8):
                mm(pso[half][:, :], yT[:, kc, i * 128:(i + 1) * 128], wo[:, kc, half * 512:(half + 1) * 512],
                   kc == 0, kc == 7, ["yT", "wo"], [psok[half]])
            act(junkO, pso[half][:, :], AF.Square, [psok[half]], ["junkO", f"ssqo{i}"],
                accum_out=ssqo[:, 2 * i + half:2 * i + half + 1])
        tt("dve", rso[:, i:i + 1], ssqo[:, 2 * i:2 * i + 1], ssqo[:, 2 * i + 1:2 * i + 2], ALU.add,
           [f"ssqo{i}"], [f"rso{i}"])
        act(rso[:, i:i + 1], rso[:, i:i + 1], AF.Ln, [f"rso{i}"], [f"rso{i}"], scale=1.0 / D, bias=EPS)
        act(rso[:, i:i + 1], rso[:, i:i + 1], AF.Exp, [f"rso{i}"], [f"rso{i}"], scale=-0.5)
        for half in range(2):
            hs = slice(half * 512, (half + 1) * 512)
            tm_, tmk = [(sg[0], "sg0"), (sg[1], "sg1"), (tmo[0], "thb0"), (tmo[1], "thb1")][(2 * i + half) % 4]
            stt("dve", tm_, pso[half][:, :], rso[:, i:i + 1], pnw2_bc[:, hs],
                ALU.mult, ALU.mult, [psok[half], f"rso{i}", "pnw2_bc"], [tmk])
            tt("pool", XT2[b][:, hs], XT2[b][:, hs], tm_, ALU.add, [f"XT2{b}", tmk], [f"XT2{b}"])
        dma(out_d[i * 128:(i + 1) * 128, :], XT2[b], [f"XT2{b}"], ["out"])

    P.finalize()
    import contextlib
    with contextlib.ExitStack() as es:
        sems = {}
        for nm in ["pe", "act", "dve", "pool"] + [f"dma{i}" for i in range(NDMA)] + [f"gdma{i}" for i in range(NGDMA)]:
            sems[nm] = es.enter_context(nc.semaphore("s_" + nm))
        block = es.enter_context(nc.Block())

        @block.tensor
        def _(e):
            P.emit("pe", e, sems)

        @block.scalar
        def _(e):
            P.emit("act", e, sems)

        @block.vector
        def _(e):
            P.emit("dve", e, sems)

        @block.gpsimd
        def _(e):
            P.emit("pool", e, sems)

        @block.sync
        def _(e):
            P.emit("dma", e, sems, final_wait=True)
    stats = {k: len(v) for k, v in P.ops.items()}
    return nc, dbg_outs, stats


_CACHE = {}


def _prep_inputs(inputs):
    f = lambda a: np.ascontiguousarray(np.asarray(a, dtype=np.float32))
    shared = {
        "pre_norm_w": f(inputs["pre_norm_w"]).reshape(8, 128),
        "mem_norm_w": f(inputs["mem_norm_w"]).reshape(8, 128),
        "w_in": f(inputs["w_in"]).reshape(D, 9232),
        "conv_w": f(inputs["conv_w"]).reshape(96, 128),
        "a_log": f(inputs["a_log"]).reshape(8, 1),
        "dt_bias": f(inputs["dt_bias"]).reshape(8, 1),
        "dn_norm_w": f(inputs["dn_norm_w"]).reshape(1, 128),
        "pool_mix_w": f(inputs["pool_mix_w"]).reshape(512, 128),
        "pool_scale": f(inputs["pool_scale"]).reshape(4, 128),
        "w_mem_kv": f(inputs["w_mem_kv"]).reshape(D, 1024),
        "w_proj_pool": f(inputs["w_proj_pool"]).reshape(512, D),
        "w_proj_delta": f(inputs["w_proj_delta"]).reshape(D, D),
        "w_proj_mem": f(inputs["w_proj_mem"]).reshape(512, D),
        "w_out": f(inputs["w_out"]).reshape(D, D),
        "post_norm_w": f(inputs["post_norm_w"]).reshape(1, D),
    }
    x = f(inputs["x"])
    mem = f(inputs["mem"])
    in_maps = []
    for b in range(8):
        m = dict(shared)
        m["x"] = x[b]
        m["mem"] = mem[b]
        in_maps.append(m)
    return in_maps


def kernel(**inputs):
    if "nc" not in _CACHE:
        _CACHE["nc"] = build_nc()
    nc, dbg_outs, stats = _CACHE["nc"]
    in_maps = _prep_inputs(inputs)
    res = run_bass_kernel_spmd(nc, in_maps, core_ids=list(range(8)))
    out = np.stack([np.asarray(res.results[b]["out"], dtype=np.float32) for b in range(8)], axis=0)
    if DEBUG:
        _CACHE["dbg"] = {k: np.asarray(res.results[0][k]) for k in dbg_outs}
    return out
```

```python
import math
import numpy as np
import concourse.bass as bass
import concourse.mybir as mybir
from concourse.bass_utils import run_bass_kernel_spmd

F32 = mybir.dt.float32
BF16 = mybir.dt.bfloat16
AF = mybir.ActivationFunctionType
ALU = mybir.AluOpType

T = 2048
D = 1024
NT = 16
EPS = 1e-6
DEBUG = False
ATTACH_WAIT = True
STOP = 99
SELF_SYNC = ("act", "dve", "pool", "dma")
NEG = -30000.0

O_XA, O_ZA, O_Q, O_K, O_V, O_A, O_B, O_ZD, O_QM, O_ZM, O_G = 0, 512, 1024, 2048, 3072, 4096, 4104, 4112, 5136, 5648, 6160


class Op:
    __slots__ = ("eng", "fn", "deps", "need_inc", "ticket", "sem", "is_dma")

    def __init__(self, eng, fn):
        self.eng = eng
        self.fn = fn
        self.deps = set()
        self.need_inc = False
        self.ticket = 0
        self.sem = eng
        self.is_dma = (eng == "dma")


NDMA = 6
NGDMA = 4


def _norm_key(k):
    if k.startswith("psR"):
        return "psR"
    if k.startswith("psE"):
        return "psE"
    return k


class Prog:
    ENGS = ("pe", "act", "dve", "pool", "dma")

    def __init__(self):
        self.ops = {e: [] for e in self.ENGS}
        self.last_w = {}
        self.readers = {}
        self.pending_barrier = {e: [] for e in self.ENGS}
        self.stopped = False
        self.dma_last = [None] * NDMA
        self.dma_n = 0
        self.gdma_last = [None] * NGDMA
        self.gdma_n = 0

    def mark(self, k):
        if STOP == k:
            self.stopped = True

    def add(self, eng, fn, r=(), w=(), skip_barrier=False, gdma=False):
        if self.stopped:
            return None
        op = Op(eng, fn)
        if gdma:
            op.is_dma = True
            slot = self.gdma_n % NGDMA
            self.gdma_n += 1
            op.sem = f"gdma{slot}"
            if self.gdma_last[slot] is not None:
                op.deps.add(self.gdma_last[slot])
            self.gdma_last[slot] = op
        r = [_norm_key(k) for k in r]
        w = [_norm_key(k) for k in w]
        w = list(dict.fromkeys(w + [k for k in r if k.startswith("ps")]))
        r = [k for k in r if not k.startswith("ps")]
        if eng == "dma":
            slot = self.dma_n % NDMA
            self.dma_n += 1
            op.sem = f"dma{slot}"
            if self.dma_last[slot] is not None:
                op.deps.add(self.dma_last[slot])
            self.dma_last[slot] = op
        if not skip_barrier:
            for d in self.pending_barrier[eng]:
                op.deps.add(d)
            self.pending_barrier[eng] = []
        for k in r:
            lw = self.last_w.get(k)
            if lw is not None:
                op.deps.add(lw)
        for k in w:
            lw = self.last_w.get(k)
            if lw is not None:
                op.deps.add(lw)
            for rd in self.readers.get(k, ()):
                op.deps.add(rd)
        for k in w:
            self.last_w[k] = op
            self.readers[k] = []
        for k in r:
            self.readers.setdefault(k, []).append(op)
        op.deps.discard(op)
        self.ops[eng].append(op)
        return op

    def barrier(self):
        lasts = [self.ops[e][-1] for e in self.ENGS if self.ops[e]]
        lasts += [o for o in self.dma_last if o is not None]
        lasts += [o for o in self.gdma_last if o is not None]
        for e in self.ENGS:
            self.pending_barrier[e] = list(lasts)

    @staticmethod
    def _needs_wait(d, eng_name):
        return d.is_dma or d.eng != eng_name or eng_name in SELF_SYNC

    def finalize(self):
        for e in self.ENGS:
            for op in self.ops[e]:
                for d in op.deps:
                    if self._needs_wait(d, e):
                        d.need_inc = True
        cnt = {}
        for e in self.ENGS:
            for op in self.ops[e]:
                if op.is_dma:
                    op.need_inc = True
                if op.need_inc:
                    cnt[op.sem] = cnt.get(op.sem, 0) + 1
                    op.ticket = cnt[op.sem] * (16 if op.is_dma else 1)
        self.final_counts = {s: c * 16 for s, c in cnt.items() if s.startswith("dma") or s.startswith("gdma")}

    def emit(self, eng_name, engine, sems, final_wait=False):
        waited = {}
        for op in self.ops[eng_name]:
            need = {}
            for d in op.deps:
                if self._needs_wait(d, eng_name):
                    if d.ticket > need.get(d.sem, 0):
                        need[d.sem] = d.ticket
            todo = [(ds, tk) for ds, tk in need.items() if tk > waited.get(ds, 0)]
            for ds, tk in todo:
                waited[ds] = tk
            attach = todo.pop() if (todo and ATTACH_WAIT and not op.is_dma
                                    and eng_name in ("act", "dve", "pool", "pe")) else None
            for ds, tk in todo:
                engine.wait_ge(sems[ds], tk)
            ins = op.fn(engine)
            if attach is not None:
                ins._wait_ge(sems[attach[0]], attach[1])
            if op.need_inc:
                ins.then_inc(sems[op.sem], 16 if op.is_dma else 1)
        if final_wait:
            for s, v in self.final_counts.items():
                engine.wait_ge(sems[s], v)


def build_nc():
    nc = bass.Bass("TRN2", target_bir_lowering=False)
    P = Prog()

    def din(name, shape):
        return nc.dram_tensor(name, list(shape), F32, kind="ExternalInput").ap()

    x_d = din("x", [T, D])
    mem_d = din("mem", [256, D])
    pnw_d = din("pre_norm_w", [8, 128])
    mnw_d = din("mem_norm_w", [8, 128])
    win_d = din("w_in", [D, 9232])
    cw_d = din("conv_w", [96, 128])
    alog_d = din("a_log", [8, 1])
    dtb_d = din("dt_bias", [8, 1])
    dnw_d = din("dn_norm_w", [1, 128])
    pmw_d = din("pool_mix_w", [512, 128])
    psc_d = din("pool_scale", [4, 128])
    wkv_d = din("w_mem_kv", [D, 1024])
    wpp_d = din("w_proj_pool", [512, D])
    wpd_d = din("w_proj_delta", [D, D])
    wpm_d = din("w_proj_mem", [512, D])
    wo_d = din("w_out", [D, D])
    pnw2_d = din("post_norm_w", [1, D])
    out_d = nc.dram_tensor("out", [T, D], F32, kind="ExternalOutput").ap()
    dbg_outs = []

    def sb(name, shape, dt):
        return nc.alloc_sbuf_tensor(name, list(shape), dt)

    hT = sb("hT", [128, 8, T], BF16)
    yaT = sb("yaT", [128, 4, T], BF16)
    ybT = sb("ybT", [128, 8, T], BF16)
    ycT = sb("ycT", [128, 4, T], BF16)
    ident_bf = sb("ident_bf", [128, 128], BF16)
    ident_f = sb("ident_f", [128, 128], F32)
    UT_f = sb("UT_f", [128, 128], F32)
    ones_f = sb("ones_f", [128, 128], F32)
    ones_bf = sb("ones_bf", [128, 128], BF16)
    NEGM = sb("NEGM", [128, 384], BF16)
    SEL2 = sb("SEL2", [4, 128], BF16)
    pstage = sb("pstage", [128, 128], F32)
    pvec = sb("pvec", [128, 128], F32)
    dnw_bc = sb("dnw_bc", [128, 128], F32)
    pnw2_bc = sb("pnw2_bc", [128, D], F32)
    alog_t = sb("alog_t", [8, 1], F32)
    dtb_t = sb("dtb_t", [8, 1], F32)
    negA = sb("negA", [8, 1], F32)
    VT = sb("VT", [128, 16, 12], BF16)
    invc = sb("invc", [128, 4, 16], F32)
    gT = sb("gT", [128, 128], F32)
    lnbT = sb("lnbT", [128, 128], F32)
    gcT = sb("gcT", [128, 128], F32)
    gtotT = sb("gtotT", [128, 128], F32)
    GLT = sb("GLT", [128, 128], F32)
    betaT = sb("betaT", [128, 128], F32)
    glastT = sb("glastT", [128, 128], F32)
    small = sb("small", [128, 256], F32)
    wstage = [sb(f"wstage{i}", [128, 1024], F32) for i in range(2)]
    NWB = 4
    wblk = [sb(f"wblk{i}", [128, 1024], BF16) for i in range(NWB)]
    thb = [sb(f"thb{i}", [128, 512], F32) for i in range(2)]
    ARENA_F32 = ((nc.sbuf_bytes_remaining - 512) // 32) * 8
    arena = sb("arena", [128, ARENA_F32], F32)

    class Carver:
        def __init__(self):
            self.off = 0

        def get(self, shape, dt):
            n = 1
            for s in shape[1:]:
                n *= s
            nbytes = n * (4 if dt == F32 else 2)
            nbytes = (nbytes + 31) // 32 * 32
            assert self.off + nbytes <= ARENA_F32 * 4, (self.off, nbytes)
            o4 = self.off // 4
            ap = arena[0:shape[0], o4:o4 + nbytes // 4]
            if dt == BF16:
                ap = ap.bitcast(BF16)
                ap = ap[:, 0:n]
            else:
                ap = ap[:, 0:n]
            self.off += nbytes
            if len(shape) == 3:
                ap = ap.rearrange("p (a b) -> p a b", a=shape[1])
            elif len(shape) == 4:
                ap = ap.rearrange("p (a b c) -> p a b c", a=shape[1], b=shape[2])
            return ap

    ps01 = [nc.alloc_psum_tensor(f"ps{i}", [128, 512], F32) for i in range(2)]
    psE = nc.alloc_psum_tensor("psE", [128, 512], F32)
    psK = nc.alloc_psum_tensor("psK", [128, 512], F32)
    psT = [nc.alloc_psum_tensor(f"psT{i}", [128, 512], F32) for i in range(2)]
    psTRb = nc.alloc_psum_tensor("psTRb", [128, 1024], BF16)
    psR = nc.alloc_psum_tensor("psR", [128, 512], F32)
    psR_b = psR[:, 384:512].bitcast(BF16)
    psTRf = psTRb[:, :].bitcast(F32)

    rot = {"ps01": 0, "ws": 0, "wb": 0}

    def next_ps():
        i = rot["ps01"]
        rot["ps01"] = 1 - i
        return ps01[i], f"ps{i}"

    def dma(out, in_, r, w, arena=True):
        P.add("dma", lambda e: e.dma_start(out=out, in_=in_), r, w, skip_barrier=not arena)

    def mm(out, lhsT, rhs, start, stop, r, w):
        P.add("pe", lambda e: e.matmul(out, lhsT=lhsT, rhs=rhs, start=start, stop=stop), r, w)

    def tr(out, in_, ident, r, w):
        P.add("pe", lambda e: e.transpose(out, in_, ident), r, w)

    def act(out, in_, func, r, w, bias=None, scale=None, accum_out=None):
        kw = {}
        if bias is not None:
            kw["bias"] = bias
        if scale is not None:
            kw["scale"] = scale
        if accum_out is not None:
            kw["accum_out"] = accum_out
        P.add("act", lambda e: e.activation(out=out, in_=in_, func=func, **kw), r, w)

    def tt(eng, out, in0, in1, op, r, w):
        P.add(eng, lambda e: e.tensor_tensor(out=out, in0=in0, in1=in1, op=op), r, w)

    def ts(eng, out, in0, s1, s2, op0, op1, r, w):
        if s2 is None:
            P.add(eng, lambda e: e.tensor_scalar(out=out, in0=in0, scalar1=s1, scalar2=None, op0=op0), r, w)
        else:
            P.add(eng, lambda e: e.tensor_scalar(out=out, in0=in0, scalar1=s1, scalar2=s2, op0=op0, op1=op1), r, w)

    def stt(eng, out, in0, scalar, in1, op0, op1, r, w):
        P.add(eng, lambda e: e.scalar_tensor_tensor(out=out, in0=in0, scalar=scalar, in1=in1, op0=op0, op1=op1), r, w)

    def cp(eng, out, in_, r, w):
        if eng == "act":
            P.add("act", lambda e: e.copy(out=out, in_=in_), r, w)
        else:
            P.add(eng, lambda e: e.tensor_copy(out=out, in_=in_), r, w)

    def memset(eng, ap, val, w):
        P.add(eng, lambda e: e.memset(ap, val), (), w)

    def asel(out, in_, pattern, cmp, fill, base, cm, r, w):
        P.add("pool", lambda e: e.affine_select(out=out, in_=in_, pattern=pattern, compare_op=cmp,
                                               fill=fill, base=base, channel_multiplier=cm), r, w)

    def load_w(src, nk, n, eng="pool", pool=None):
        si = rot["ws"]
        rot["ws"] = 1 - si
        if pool is None:
            bi = rot["wb"]
            rot["wb"] = (bi + 1) % NWB
            wbuf, wkey = wblk[bi], f"wblk{bi}"
        else:
            wbuf, wkey = pool["bufs"][pool["i"]]
            pool["i"] = (pool["i"] + 1) % len(pool["bufs"])
        st = wstage[si][:, 0:nk * n].rearrange("p (k n) -> p k n", k=nk)
        wb = wbuf[:, 0:nk * n].rearrange("p (k n) -> p k n", k=nk)
        dma(st, src.rearrange("(k p) n -> p k n", p=128), [], [f"wstage{si}"], arena=False)
        cp(eng, wb, st, [f"wstage{si}"], [wkey])
        return wb, wkey

    thi = {"i": 0}

    def silu2(dst, ps_ap, pk, dkey):
        i = thi["i"]
        thi["i"] = 1 - i
        act(thb[i][:, :], ps_ap, AF.Tanh, [pk], [f"thb{i}"], scale=0.5)
        stt("dve", dst, thb[i][:, :], 1.0, ps_ap, ALU.add, ALU.mult, [f"thb{i}", pk], [dkey])

    def dbg(name, ap, shape, key):
        if not DEBUG:
            return
        dt = ap.dtype
        d = nc.dram_tensor("dbg_" + name, list(shape), dt, kind="ExternalOutput").ap()
        dma(d, ap, [key], ["dbg_" + name])
        dbg_outs.append("dbg_" + name)

    memset("pool", ident_bf[:], 0.0, ["ident_bf"])
    asel(ident_bf[:], ident_bf[:], [[-1, 128]], ALU.not_equal, 1.0, 0, 1, ["ident_bf"], ["ident_bf"])
    memset("pool", ident_f[:], 0.0, ["ident_f"])
    asel(ident_f[:], ident_f[:], [[-1, 128]], ALU.not_equal, 1.0, 0, 1, ["ident_f"], ["ident_f"])
    memset("pool", UT_f[:], 1.0, ["UT_f"])
    asel(UT_f[:], UT_f[:], [[1, 128]], ALU.is_ge, 0.0, 0, -1, ["UT_f"], ["UT_f"])
    memset("pool", ones_f[:], 1.0, ["ones_f"])
    memset("pool", ones_bf[:], 1.0, ["ones_bf"])
    memset("pool", NEGM[:], 0.0, ["NEGM"])
    asel(NEGM[:, 0:128], NEGM[:, 0:128], [[-1, 128]], ALU.is_gt, NEG, 0, 1, ["NEGM"], ["NEGM"])
    asel(NEGM[:, 128:256], NEGM[:, 128:256], [[1, 128]], ALU.is_gt, NEG, 0, -1, ["NEGM"], ["NEGM"])
    asel(NEGM[:, 256:384], NEGM[:, 256:384], [[1, 128]], ALU.is_ge, NEG, 0, -1, ["NEGM"], ["NEGM"])
    memset("pool", SEL2[:], 0.0, ["SEL2"])
    memset("pool", SEL2[0:2, :], 1.0, ["SEL2"])
    memset("pool", VT[:], 1.0, ["VT"])
    memset("pool", pstage[:], 0.0, ["pstage"])
    for g, wdw in enumerate((2, 4, 8, 16)):
        memset("pool", invc[:, g, :], 1.0 / wdw, ["invc"])
        for t_ in range(wdw - 1):
            memset("pool", invc[:, g, t_:t_ + 1], 1.0 / (t_ + 1), ["invc"])

    dma(pstage[0:8, :], pnw_d, [], ["pstage"])
    dma(pstage[8:16, :], mnw_d, [], ["pstage"])
    dma(pstage[16:112, :], cw_d, [], ["pstage"])
    dma(pstage[112:116, :], psc_d, [], ["pstage"])
    dma(alog_t[:], alog_d, [], ["alog_t"])
    dma(dtb_t[:], dtb_d, [], ["dtb_t"])
    dma(dnw_bc[:], dnw_d.partition_broadcast(128), [], ["dnw_bc"])
    ts("dve", dnw_bc[:], dnw_bc[:], 0.5, None, ALU.mult, None, ["dnw_bc"], ["dnw_bc"])
    dma(pnw2_bc[:], pnw2_d.partition_broadcast(128), [], ["pnw2_bc"])
    tr(psR[:, 0:128], pstage[:], ident_f[:], ["pstage", "ident_f"], ["psR0"])
    cp("dve", pvec[:], psR[:, 0:128], ["psR0"], ["pvec"])
    act(negA[:], alog_t[:], AF.Exp, ["alog_t"], ["negA"])
    ts("dve", negA[:], negA[:], -1.0, None, ALU.mult, None, ["negA"], ["negA"])

    PV_PNW, PV_MNW, PV_CW, PV_PSC = 0, 8, 16, 112
    ts("dve", pvec[:, PV_PSC:PV_PSC + 4], pvec[:, PV_PSC:PV_PSC + 4], 0.5, None, ALU.mult, None, ["pvec"], ["pvec"])
    dbg("pvec", pvec[:, :], [128, 128], "pvec")
    P.mark(0)

    cv = Carver()
    memnT = cv.get([128, 8, 256], BF16)
    MEMN_END = cv.off
    XT = [cv.get([128, D], F32) for _ in range(2)]
    XN = [cv.get([128, D], BF16) for _ in range(2)]
    junk0 = cv.get([128, D], BF16)
    ssq0 = small[:, 0:32]
    rstd0 = small[:, 32:64]

    def norm_tile(src_ap, i, dst, dst_key, col0, pv0):
        b = i % 2
        dma(XT[b], src_ap, [], [f"XT{b}"])
        act(junk0, XT[b], AF.Square, [f"XT{b}"], ["junk0", f"ssq0_{i}"], accum_out=ssq0[:, i:i + 1])
        act(rstd0[:, i:i + 1], ssq0[:, i:i + 1], AF.Ln, [f"ssq0_{i}"], [f"rstd0_{i}"], scale=1.0 / D, bias=EPS)
        act(rstd0[:, i:i + 1], rstd0[:, i:i + 1], AF.Exp, [f"rstd0_{i}"], [f"rstd0_{i}"], scale=-0.5)
        act(XN[b], XT[b], AF.Copy, [f"XT{b}", f"rstd0_{i}"], [f"XN{b}"], scale=rstd0[:, i:i + 1])
        for c in range(8):
            tr(psTRb[:, c * 128:(c + 1) * 128], XN[b][:, c * 128:(c + 1) * 128], ident_bf[:],
               [f"XN{b}", "ident_bf"], ["psTRb"])
        tt("dve", dst[:, :, col0:col0 + 128], psTRb[:, :].rearrange("p (c n) -> p c n", c=8),
           pvec[:, pv0:pv0 + 8].unsqueeze(2).to_broadcast([128, 8, 128]), ALU.mult,
           ["psTRb", "pvec"], [dst_key])

    for i in range(NT):
        norm_tile(x_d[i * 128:(i + 1) * 128, :], i, hT, "hT", i * 128, PV_PNW)
    for i in range(2):
        norm_tile(mem_d[i * 128:(i + 1) * 128, :], NT + i, memnT, "memnT", i * 128, PV_MNW)
    dbg("hT", hT[:, :, :], [128, 8, T], "hT")
    P.mark(1)

    def inproj(wb, wkey, nk_list, blk, M, ps, pskey, rhs_src=None, rhs_key="hT"):
        src = hT if rhs_src is None else rhs_src
        n = len(nk_list)
        for ii, kc in enumerate(nk_list):
            mm(ps[0:M, 0:512], wb[:, ii, 0:M], src[:, kc, blk * 512:(blk + 1) * 512], ii == 0, ii == n - 1,
               [wkey, rhs_key], [pskey])

    P.barrier()
    cv = Carver()
    cv.off = MEMN_END
    g_fm = cv.get([8, T], F32)
    lnb_fm = cv.get([8, T], F32)
    tmpa = cv.get([8, 512], F32)
    tmpb = cv.get([8, 512], F32)
    wab, wabk = load_w(win_d[:, O_A:O_A + 16], 8, 16)
    for blk in range(4):
        bs = slice(blk * 512, (blk + 1) * 512)
        ps, pk = next_ps()
        for kc in range(8):
            mm(ps[0:8, :], wab[:, kc, 0:8], hT[:, kc, bs], kc == 0, kc == 7, [wabk, "hT"], [pk])
        act(tmpa, ps[0:8, :], AF.Exp, [pk, "dtb_t"], ["tmpa"], bias=dtb_t[:])
        ps2, pk2 = next_ps()
        for kc in range(8):
            mm(ps2[0:8, :], wab[:, kc, 8:16], hT[:, kc, bs], kc == 0, kc == 7, [wabk, "hT"], [pk2])
        act(tmpb, ps2[0:8, :], AF.Exp, [pk2], ["tmpb"], scale=-1.0)
        act(tmpa, tmpa, AF.Ln, ["tmpa"], ["tmpa"], bias=1.0)
        act(tmpb, tmpb, AF.Ln, ["tmpb"], ["tmpb"], bias=1.0)
        ts("dve", g_fm[:, bs], tmpa, negA[:], None, ALU.mult, None, ["tmpa", "negA"], ["g_fm"])
        ts("dve", lnb_fm[:, bs], tmpb, -1.0, None, ALU.mult, None, ["tmpb"], ["lnb_fm"])
    for c in range(NT):
        tr(psR[:, c * 8:(c + 1) * 8], g_fm[:, c * 128:(c + 1) * 128], ident_f[0:8, 0:8], ["g_fm", "ident_f"], ["psR0"])
    cp("dve", gT[:], psR[:, 0:128], ["psR0"], ["gT"])
    for c in range(NT):
        tr(psR[:, 128 + c * 8:128 + (c + 1) * 8], lnb_fm[:, c * 128:(c + 1) * 128], ident_f[0:8, 0:8],
           ["lnb_fm", "ident_f"], ["psR1"])
    cp("dve", lnbT[:], psR[:, 128:256], ["psR1"], ["lnbT"])
    mm(psR[:, 256:384], UT_f[:], gT[:], True, True, ["UT_f", "gT"], ["psR2"])
    cp("dve", gcT[:], psR[:, 256:384], ["psR2"], ["gcT"])
    mm(psE[:, 0:128], ones_f[:], gT[:], True, True, ["ones_f", "gT"], ["psE"])
    cp("dve", gtotT[:], psE[:, 0:128], ["psE"], ["gtotT"])
    tt("dve", GLT[:], gcT[:], lnbT[:], ALU.add, ["gcT", "lnbT"], ["GLT"])
    act(betaT[:], lnbT[:], AF.Exp, ["lnbT"], ["betaT"], bias=-math.log(2.0))
    act(glastT[:], gtotT[:], AF.Exp, ["gtotT"], ["glastT"])

    dbg("gcT", gcT[:, :], [128, 128], "gcT")
    dbg("betaT", betaT[:, :], [128, 128], "betaT")
    P.mark(2)
    P.barrier()
    cv = Carver()
    cv.off = MEMN_END
    raw = [cv.get([128, T + 8], BF16) for _ in range(2)]
    qf = cv.get([128, T], BF16)
    kf = cv.get([128, T], BF16)
    vf = cv.get([128, T], BF16)
    szd2 = [cv.get([128, T], BF16), yaT[:, 1, :]]
    qgT2 = [cv.get([128, T], BF16), yaT[:, 0, :]]
    sqb = raw[0][:, 8:8 + T]
    dg = cv.get([128, 4, 128], BF16)
    PF = cv.get([4, 16, 2, 128], BF16)
    E2 = cv.get([4, 16, 128], BF16)
    Eq = thb[0][:, :]
    UB = 8

    def u3(ap2d):
        return ap2d.rearrange("p (u n) -> p u n", u=UB)

    OPS = [
        dict(kbg=cv.get([128, UB, 128], BF16), kd=cv.get([128, UB, 128], BF16), vb=cv.get([128, UB, 128], BF16),
             TT=cv.get([128, UB, 128], BF16), nwT=cv.get([128, UB, 128], BF16), aqk=cv.get([128, UB, 128], BF16)),
        dict(kbg=u3(yaT[:, 2, 0:1024]), kd=u3(yaT[:, 2, 1024:2048]), vb=u3(yaT[:, 3, 0:1024]),
             TT=u3(yaT[:, 3, 1024:2048]), nwT=u3(ycT[:, 0, 0:1024]), aqk=u3(ycT[:, 0, 1024:2048])),
    ]
    GU = 4
    NSLOT = 8
    TS = [[cv.get([128, 384], BF16) for _ in range(2)] for _ in range(GU)]
    Dall = [cv.get([128, 384], F32)] * 2
    AB0 = [cv.get([128, 256], BF16) for _ in range(GU)]
    Xa = [cv.get([128, 128], BF16) for _ in range(GU)]
    Rb = [cv.get([128, 128], BF16) for _ in range(GU)]
    XTn = [cv.get([128, 128], BF16) for _ in range(GU)]
    IA0 = [cv.get([128, 128], BF16) for _ in range(GU)]
    ycx = ycT[:, 1:4, :].rearrange("p a b -> p (a b)")
    for sl_ in range(4):
        o_ = sl_ * 1536
        AB0.append(ycx[:, o_:o_ + 256])
        TS.append([ycx[:, o_ + 256:o_ + 640], ycx[:, o_ + 640:o_ + 1024]])
        IA0.append(ycx[:, o_ + 1024:o_ + 1152])
        Xa.append(ycx[:, o_ + 1152:o_ + 1280])
        Rb.append(ycx[:, o_ + 1280:o_ + 1408])
        XTn.append(ycx[:, o_ + 1408:o_ + 1536])
    otok = cv.get([128, 16, 128], BF16)
    onb = [cv.get([128, 128], BF16) for _ in range(2)]
    S_f = cv.get([128, 128], F32)
    S_b = cv.get([128, 128], BF16)
    vnew_b = cv.get([128, 128], BF16)
    X3 = cv.get([128, 16, 3], F32)
    Lq = cv.get([128, 16], F32)
    Lk = cv.get([128, 16], F32)
    sc_kbg = cv.get([128, 16], F32)
    sc_kd = cv.get([128, 16], F32)
    tmp16 = cv.get([128, 16], F32)
    ossq = cv.get([128, 16], F32)
    orstd = cv.get([128, 16], F32)
    junkD = cv.get([128, 128], BF16)
    print("delta arena bytes used", cv.off, "of", ARENA_F32 * 4)

    for rb in range(2):
        memset("pool", raw[rb][:, 0:3], 0.0, [f"raw{rb}"])

    gcT3 = gcT[:, :].rearrange("p (c h) -> p c h", h=8)
    GLT3 = GLT[:, :].rearrange("p (c h) -> p c h", h=8)
    gtotT3 = gtotT[:, :].rearrange("p (c h) -> p c h", h=8)
    betaT3 = betaT[:, :].rearrange("p (c h) -> p c h", h=8)

    ev = {"i": 0}

    def evac(out, in_, r, w):
        ev["i"] ^= 1
        cp("act" if ev["i"] else "dve", out, in_, r, w)

    def s1x(h, which):
        p = h % 2
        lst = ((0, O_Q, qf, "qf", 0), (1, O_K, kf, "kf", 1)) if which == "qk" else ((2, O_V, vf, "vf", 1),)
        for t_i, c0, dst, dkey, ri in lst:
            wb, wk = load_w(win_d[:, c0 + h * 128:c0 + (h + 1) * 128], 8, 128)
            rw = raw[ri]
            rk = f"raw{ri}"
            for blk in range(4):
                ps, pk = next_ps()
                inproj(wb, wk, range(8), blk, 128, ps, pk)
                cp("act", rw[:, 3 + blk * 512:3 + (blk + 1) * 512], ps[:, :], [pk], [rk])
                yield
            ch = t_i * 8 + h
            for j in range(4):
                col = PV_CW + j * 24 + ch
                ts("dve", dg[:, j, :], ident_bf[:], pvec[:, col:col + 1], None, ALU.mult, None,
                   ["ident_bf", "pvec"], ["dg"])
            for blk in range(4):
                ps, pk = next_ps()
                for j in range(4):
                    mm(ps[:, :], dg[:, j, :], rw[:, j + blk * 512:j + blk * 512 + 512], j == 0, j == 3,
                       ["dg", rk], [pk])
                silu2(dst[:, blk * 512:(blk + 1) * 512], ps[:, :], pk, dkey)
                yield
        if which == "vz":
            wb, wk = load_w(win_d[:, O_ZD + h * 128:O_ZD + (h + 1) * 128], 8, 128)
            for blk in range(4):
                ps, pk = next_ps()
                inproj(wb, wk, range(8), blk, 128, ps, pk)
                silu2(szd2[p][:, blk * 512:(blk + 1) * 512], ps[:, :], pk, f"szd{p}")
                yield
    N_S1QK = 16
    N_S1VZ = 12
    N_S1 = 28

    def s2(h):
        p = h % 2
        act(sqb, qf, AF.Square, ["qf"], ["raw0"])
        yield
        for c in range(NT):
            mm(psR[:, c:c + 1], sqb[:, c * 128:(c + 1) * 128], ones_bf[:, 0:1], True, True, ["raw0", "ones_bf"], ["psR0"])
        act(Lq, psR[:, 0:16], AF.Ln, ["psR0"], ["Lq"], scale=128.0, bias=128.0 * 4.0 * EPS)
        yield
        act(sqb, kf, AF.Square, ["kf"], ["raw0"])
        yield
        for c in range(NT):
            mm(psR[:, 128 + c:128 + c + 1], sqb[:, c * 128:(c + 1) * 128], ones_bf[:, 0:1], True, True,
               ["raw0", "ones_bf"], ["psR1"])
        act(Lk, psR[:, 128:144], AF.Ln, ["psR1"], ["Lk"], bias=4.0 * EPS)
        yield
        stt("dve", X3[:, :, 0], Lk, -0.5, GLT3[:, :, h], ALU.mult, ALU.add, ["Lk", "GLT"], ["X3"])
        stt("dve", X3[:, :, 1], Lq, -0.5, gcT3[:, :, h], ALU.mult, ALU.add, ["Lq", "gcT"], ["X3"])
        stt("dve", X3[:, :, 2], Lk, -0.5, gcT3[:, :, h], ALU.mult, ALU.subtract, ["Lk", "gcT"], ["X3"])
        act(sc_kbg, X3[:, :, 0], AF.Exp, ["X3"], ["sc_kbg"])
        tt("dve", tmp16, X3[:, :, 2], gtotT3[:, :, h], ALU.add, ["X3", "gtotT"], ["tmp16"])
        act(sc_kd, tmp16, AF.Exp, ["tmp16"], ["sc_kd"])
        cp("dve", VT[:, :, 0:8:4], X3[:, :, 0:2], ["X3"], ["VT"])
        cp("dve", VT[:, :, 10:11], X3[:, :, 2:3], ["X3"], ["VT"])
        tt("dve", VT[:, :, 1:9:4], X3[:, :, 0:2], VT[:, :, 0:8:4], ALU.subtract, ["X3", "VT"], ["VT"])
        tt("dve", VT[:, :, 11:12], X3[:, :, 2:3], VT[:, :, 10:11], ALU.subtract, ["X3", "VT"], ["VT"])
        yield
        for gq in range(4):
            for cc in range(4):
                c = gq * 4 + cc
                for v in range(2):
                    o = (cc * 2 + v) * 128
                    tr(psTRb[0:4, o:o + 128], VT[:, c, v * 4:v * 4 + 4], ident_bf[:], ["VT", "ident_bf"], ["psTRb"])
            evac(PF[:, gq * 4:(gq + 1) * 4, :, :], psTRb[0:4, :].rearrange("p (c v n) -> p c v n", c=4, v=2),
                 ["psTRb"], ["PF"])
            yield
        for g8 in range(2):
            for cc in range(8):
                c = g8 * 8 + cc
                tr(psTRb[0:4, cc * 128:(cc + 1) * 128], VT[:, c, 8:12], ident_bf[:], ["VT", "ident_bf"], ["psTRb"])
            evac(E2[:, g8 * 8:(g8 + 1) * 8, :], psTRb[0:4, :].rearrange("p (c n) -> p c n", c=8), ["psTRb"], ["E2"])
            yield
        for blk in range(4):
            ps, pk = next_ps()
            mm(ps[:, :], SEL2[:, :], PF[:, blk * 4:(blk + 1) * 4, 1, :], True, True, ["SEL2", "PF"], [pk])
            act(Eq, ps[:, :], AF.Exp, [pk], ["thb0"])
            tt("dve", qgT2[p][:, blk * 512:(blk + 1) * 512], qf[:, blk * 512:(blk + 1) * 512], Eq, ALU.mult,
               ["qf", "thb0"], [f"qgT{p}"])
            yield
    N_S2 = 15

    TB = [(psT[0], "psT0"), (psT[1], "psT1"), (ps01[0], "ps0"), (ps01[1], "ps1")]
    NST = 6

    def S3(h):
        def slot_of(c):
            return c % NSLOT

        def setup_unit(c):
            ub = c // UB
            O = OPS[ub]
            so = ub
            lc = c - ub * UB
            u = slot_of(c)
            cs = slice(c * 128, (c + 1) * 128)
            tr(psR_b[:, 0:128], kf[:, cs], ident_bf[:], ["kf", "ident_bf"], ["psR3"])
            tr(psR_b[:, 128:256], vf[:, cs], ident_bf[:], ["vf", "ident_bf"], ["psR3"])
            P.add("act", (lambda o_, i_, s_: (lambda e: e.activation(out=o_, in_=i_, func=AF.Copy, scale=s_)))(
                O["kbg"][:, lc, :], psR_b[:, 0:128], sc_kbg[:, c:c + 1]), ["psR3", "sc_kbg"], [f"kbg{so}_{lc}"])
            ts("dve", O["kd"][:, lc, :], psR_b[:, 0:128], sc_kd[:, c:c + 1], None, ALU.mult, None,
               ["psR3", "sc_kd"], [f"kd{so}_{lc}"])
            ts("dve", O["vb"][:, lc, :], psR_b[:, 128:256], betaT3[:, c, h:h + 1], None, ALU.mult, None,
               ["psR3", "betaT"], [f"vb{so}_{lc}"])
            mm(psE[:, 0:384], ident_bf[:], NEGM[:, 0:384], True, False, ["ident_bf", "NEGM"], ["psE"])
            mm(psE[:, 0:128], PF[:, c, 0, :], E2[:, c, :], False, False, ["PF", "E2"], ["psE"])
            mm(psE[:, 128:384], E2[:, c, :], PF[:, c, :, :], False, True, ["PF", "E2"], ["psE"])
            Dl = Dall[c % 2]
            dk_ = "Dall"
            act(Dl, psE[:, 0:384], AF.Exp, ["psE"], [dk_])
            mm(psK[:, 0:128], kf[:, cs], kf[:, cs], True, True, ["kf"], ["psK"])
            mm(psK[:, 128:256], kf[:, cs], kf[:, cs], True, True, ["kf"], ["psK"])
            mm(psK[:, 256:384], kf[:, cs], qf[:, cs], True, True, ["kf", "qf"], ["psK"])
            tsb = TS[u][1]
            tt("dve", AB0[u][:, 0:256], psK[:, 0:256], Dl[:, 0:256], ALU.mult, ["psK", dk_], [f"AB0{u}"])
            tt("dve", O["aqk"][:, lc, :], psK[:, 256:384], Dl[:, 256:384], ALU.mult, ["psK", dk_], [f"aqk{so}_{lc}"])
            tt("pool", tsb[:, 256:384], ident_bf[:], AB0[u][:, 128:256], ALU.subtract,
               ["ident_bf", f"AB0{u}"], [f"TS{u}b"])
            tt("pool", IA0[u], ident_bf[:], AB0[u][:, 0:128], ALU.add, ["ident_bf", f"AB0{u}"], [f"IA0{u}"])

        def iter_step(g, s):
            for i_ in range(GU):
                c = g * GU + i_
                u = slot_of(c)
                cur, nxt = (TS[u][0], TS[u][1]) if s % 2 == 0 else (TS[u][1], TS[u][0])
                ck = f"TS{u}a" if s % 2 == 0 else f"TS{u}b"
                if s == 0:
                    cur, ck = AB0[u], f"AB0{u}"
                nk_ = f"TS{u}b" if s % 2 == 0 else f"TS{u}a"
                pt, ptk = TB[i_]
                if s == 0:
                    mm(pt[:, 0:128], cur[:, 128:256], cur[:, 0:128], True, True, [ck], [ptk])
                    mm(pt[:, 128:256], cur[:, 0:128], cur[:, 128:256], True, True, [ck], [ptk])
                elif s < NST - 1:
                    mm(pt[:, 0:128], cur[:, 128:256], cur[:, 0:128], True, True, [ck], [ptk])
                    mm(pt[:, 128:384], cur[:, 0:128], cur[:, 128:384], True, False, [ck], [ptk])
                    mm(pt[:, 256:384], ident_bf[:], cur[:, 256:384], False, True, [ck, "ident_bf"], [ptk])
                else:
                    mm(pt[:, 256:384], cur[:, 0:128], cur[:, 256:384], True, False, [ck], [ptk])
                    mm(pt[:, 256:384], ident_bf[:], cur[:, 256:384], False, True, [ck, "ident_bf"], [ptk])
                if s == 0:
                    evac(nxt[:, 0:256], pt[:, 0:256], [ptk], [nk_])
                elif s < NST - 1:
                    evac(nxt[:, 0:384], pt[:, 0:384], [ptk], [nk_])
                else:
                    evac(Xa[u], pt[:, 256:384], [ptk], [f"Xa{u}"])

        def fin_parts(c):
            ub = c // UB
            O = OPS[ub]
            so = ub
            lc = c - ub * UB
            u = slot_of(c)

            def p1():
                mm(psTRf[:, 256:384], IA0[u], Xa[u], True, True, [f"IA0{u}", f"Xa{u}"], ["psTRb"])
                tr(psTRb[:, 0:128], Xa[u], ident_bf[:], [f"Xa{u}", "ident_bf"], ["psTRb"])
                stt("dve", Rb[u], ident_bf[:], 2.0, psTRf[:, 256:384], ALU.mult, ALU.subtract,
                    ["ident_bf", "psTRb"], [f"Rb{u}"])
                cp("act", XTn[u], psTRb[:, 0:128], ["psTRb"], [f"XTn{u}"])

            def p2():
                mm(psTRf[:, 384:512], XTn[u], Rb[u], True, True, [f"XTn{u}", f"Rb{u}"], ["psTRb"])
                evac(O["TT"][:, lc, :], psTRf[:, 384:512], ["psTRb"], [f"TT{so}_{lc}"])

            def p3():
                mm(psE[:, 384:512], O["kbg"][:, lc, :], O["TT"][:, lc, :], True, True,
                   [f"kbg{so}_{lc}", f"TT{so}_{lc}"], ["psEw"])
                ts("dve", O["nwT"][:, lc, :], psE[:, 384:512], -1.0, None, ALU.mult, None, ["psEw"], [f"nwT{so}_{lc}"])
            return [p1, p2, p3]

        NG = NT // GU
        for i_ in range(GU):
            setup_unit(i_)
            yield
        for g in range(NG):
            fins = [fin_parts(c) for c in range((g - 1) * GU, g * GU)] if g > 0 else None
            for s in range(NST):
                iter_step(g, s)
                yield
                if fins is not None:
                    if 1 <= s <= GU:
                        fins[s - 1][1]()
                    if 2 <= s <= GU + 1:
                        fins[s - 2][2]()
                    if s < GU:
                        fins[s][0]()
                    yield
                if g + 1 < NG and 2 <= s <= GU + 1:
                    setup_unit((g + 1) * GU + s - 2)
                    yield
        for c in range((NG - 1) * GU, NG * GU):
            for f_ in fin_parts(c):
                f_()
                yield
    N_S3H = 4 + 4 * 6 + 3 * 4 + 4 * 12

    def s4(h, ub, so):
        p = h % 2
        O = OPS[so]
        for lc in range(UB):
            c = ub * UB + lc
            cs = slice(c * 128, (c + 1) * 128)
            first = (c == 0)
            mm(psR[:, 0:128], O["TT"][:, lc, :], O["vb"][:, lc, :], True, first,
               [f"TT{so}_{lc}", f"vb{so}_{lc}"], ["psR0"])
            if not first:
                mm(psR[:, 0:128], O["nwT"][:, lc, :], S_b, False, True, [f"nwT{so}_{lc}", "S_b"], ["psR0"])
            cp("act", vnew_b, psR[:, 0:128], ["psR0"], ["vnew_b"])
            yield
            mm(psE[:, 0:128], O["kd"][:, lc, :], vnew_b, True, True, [f"kd{so}_{lc}", "vnew_b"], ["psE"])
            if not first:
                mm(psK[:, 0:128], qgT2[p][:, cs], S_b, True, False, [f"qgT{p}", "S_b"], ["psK"])
            mm(psK[:, 0:128], O["aqk"][:, lc, :], vnew_b, first, True, [f"aqk{so}_{lc}", "vnew_b"], ["psK"])
            if first:
                cp("dve", S_b, psE[:, 0:128], ["psE"], ["S_b"])
                cp("dve", S_f, psE[:, 0:128], ["psE"], ["S_f"])
            else:
                gl = glastT[:, c * 8 + h:c * 8 + h + 1]
                stt("dve", S_b, S_f, gl, psE[:, 0:128], ALU.mult, ALU.add, ["S_f", "glastT", "psE"], ["S_b"])
                stt("dve", S_f, S_f, gl, psE[:, 0:128], ALU.mult, ALU.add, ["S_f", "glastT", "psE"], ["S_f"])
            act(junkD, psK[:, 0:128], AF.Square, ["psK"], ["junkD", "ossq"], accum_out=ossq[:, c:c + 1])
            cp("dve", otok[:, c, :], psK[:, 0:128], ["psK"], ["otok"])
            yield
    N_S4 = 16

    def s5(h):
        p = h % 2
        act(orstd, ossq, AF.Ln, ["ossq"], ["orstd"], scale=1.0 / 128, bias=EPS)
        act(orstd, orstd, AF.Exp, ["orstd"], ["orstd"], scale=-0.5)
        for c in range(NT):
            stt("dve", otok[:, c, :], otok[:, c, :], orstd[:, c:c + 1], dnw_bc[:], ALU.mult, ALU.mult,
                ["otok", "orstd", "dnw_bc"], ["otok"])
            if c % 4 == 3:
                yield
        for blk in range(4):
            for cc in range(4):
                c = blk * 4 + cc
                tr(psTRb[:, cc * 128:(cc + 1) * 128], otok[:, c, :], ident_bf[:], ["otok", "ident_bf"], ["psTRb"])
            tt("dve", ybT[:, h, blk * 512:(blk + 1) * 512], psTRb[:, 0:512], szd2[p][:, blk * 512:(blk + 1) * 512],
               ALU.mult, ["psTRb", f"szd{p}"], ["ybT"])
            yield
    N_S5 = 8

    def run(g):
        for _ in g:
            pass

    def chain(*gs):
        for g in gs:
            yield from g

    def par(ga, na, gb, nb):
        da = db = 0
        alive_a = alive_b = True
        while alive_a or alive_b:
            pick_a = alive_a and (not alive_b or da * nb <= db * na)
            if pick_a:
                try:
                    next(ga)
                    da += 1
                except StopIteration:
                    alive_a = False
            else:
                try:
                    next(gb)
                    db += 1
                except StopIteration:
                    alive_b = False

    def par(*pairs):
        gens = [[pairs[i], pairs[i + 1], 0, True] for i in range(0, len(pairs), 2)]
        while any(g[3] for g in gens):
            best = None
            for g in gens:
                if g[3] and (best is None or g[2] * best[1] < best[2] * g[1]):
                    best = g
            try:
                next(best[0])
                best[2] += 1
            except StopIteration:
                best[3] = False

    def gpar(*pairs):
        gens = [[pairs[i], pairs[i + 1], 0, True] for i in range(0, len(pairs), 2)]
        while any(g[3] for g in gens):
            best = None
            for g in gens:
                if g[3] and (best is None or g[2] * best[1] < best[2] * g[1]):
                    best = g
            try:
                next(best[0])
                best[2] += 1
                yield
            except StopIteration:
                best[3] = False

    run(s1x(0, "qk"))
    run(s1x(0, "vz"))
    run(s2(0))
    for h in range(8):
        if h == 0:
            run(S3(0))
        elif h < 7:
            par(S3(h), N_S3H, chain(s4(h - 1, 1, 1), s5(h - 1)), 2 * (N_S4 + N_S5))
        else:
            par(S3(h), N_S3H, chain(s4(h - 1, 1, 1), s5(h - 1), s4(h, 0, 0)), 2 * (N_S4 + N_S5))
        if h < 7:
            par(chain(s1x(h + 1, "qk"), gpar(s1x(h + 1, "vz"), N_S1VZ, s2(h + 1), N_S2)), N_S1QK + N_S1VZ + N_S2,
                s4(h, 0, 0), N_S4)
    run(s4(7, 1, 1))
    run(s5(7))
    dbg("ybT", ybT[:, :, :], [128, 8, T], "ybT")
    P.mark(3)

    P.barrier()
    cv = Carver()
    cv.off = MEMN_END
    A_ = dict(U=cv.get([128, 16 + T], F32), S1=cv.get([128, 16 + T], F32), S2=cv.get([128, 16 + T], F32),
              sza=cv.get([128, T], BF16), pTb=cv.get([128, T], BF16), ptmp=cv.get([128, 16], F32))
    for nm in ("U", "S1", "S2"):
        memset("pool", A_[nm][:, 0:16], 0.0, [f"{nm}A"])
    poolA = {"bufs": [(cv.get([128, 1024], BF16), f"wA{i}") for i in range(3)], "i": 0}
    poolC = {"bufs": [(cv.get([128, 1024], BF16), f"wC{i}") for i in range(4)], "i": 0}
    kmT = cv.get([128, 4, 256], BF16)
    vm = cv.get([128, 2, 512], BF16)
    qmT2 = [cv.get([128, 512], BF16) for _ in range(2)]
    szm2 = [cv.get([128, 512], BF16) for _ in range(2)]
    pT4 = [[cv.get([128, 512], BF16) for _ in range(2)] for _ in range(2)]
    rden2 = [cv.get([128, 512], F32) for _ in range(2)]
    tnum2 = [cv.get([128, 512], F32) for _ in range(2)]
    print("A+C arena bytes used", cv.off, "of", ARENA_F32 * 4)

    def phaseA():
        U, S1, S2, sza, pTb, ptmp = A_["U"], A_["S1"], A_["S2"], A_["sza"], A_["pTb"], A_["ptmp"]
        kU, kS1, kS2, ksza, kpT, kpt = ("UA", "S1A", "S2A", "szaA", "pTbA", "ptmpA")
        for g, wdw in enumerate((2, 4, 8, 16)):
            wb, wk = load_w(win_d[:, O_XA + g * 128:O_XA + (g + 1) * 128], 8, 128, pool=poolA, eng="act")
            for blk in range(4):
                ps, pk = next_ps()
                inproj(wb, wk, range(8), blk, 128, ps, pk)
                cp("act", U[:, 16 + blk * 512:16 + (blk + 1) * 512], ps[:, :], [pk], [kU])
                yield
            wb, wk = load_w(win_d[:, O_ZA + g * 128:O_ZA + (g + 1) * 128], 8, 128, pool=poolA, eng="act")
            for blk in range(4):
                ps, pk = next_ps()
                inproj(wb, wk, range(8), blk, 128, ps, pk)
                silu2(sza[:, blk * 512:(blk + 1) * 512], ps[:, :], pk, ksza)
                yield
            wm, wmk = load_w(pmw_d[g * 128:(g + 1) * 128, :], 1, 128, pool=poolA, eng="act")
            src_, sk = U, kU
            sh = 1
            bufs = [(S1, kS1), (S2, kS2)]
            bi = 0
            while sh < wdw:
                dst, dk2 = bufs[bi]
                bi ^= 1
                tt("dve", dst[:, 16:16 + T], src_[:, 16:16 + T], src_[:, 16 - sh:16 - sh + T], ALU.add, [sk], [dk2])
                src_, sk = dst, dk2
                sh *= 2
                yield
            stt("dve", pTb[:, 16:T], src_[:, 32:16 + T], 1.0 / wdw, U[:, 32:16 + T], ALU.mult, ALU.subtract,
                [sk, kU], [kpT])
            tt("dve", ptmp, src_[:, 16:32], invc[:, g, :], ALU.mult, [sk, "invc"], [kpt])
            tt("dve", pTb[:, 0:16], ptmp, U[:, 16:32], ALU.subtract, [kpt, kU], [kpT])
            yield
            for blk in range(4):
                ps, pk = next_ps()
                mm(ps[:, :], wm[:, 0, :], pTb[:, blk * 512:(blk + 1) * 512], True, True, [wmk, kpT], [pk])
                stt("dve", yaT[:, g, blk * 512:(blk + 1) * 512], ps[:, :], pvec[:, PV_PSC + g:PV_PSC + g + 1],
                    sza[:, blk * 512:(blk + 1) * 512], ALU.mult, ALU.mult, [pk, "pvec", ksza], ["yaT"])
                yield
    N_A = 4 * (4 + 4 + 3 + 1 + 4)

    def phaseC():
        ci_ = 0
        for hh in range(4):
            wb, wk = load_w(wkv_d[:, hh * 128:(hh + 1) * 128], 8, 128, pool=poolC, eng="act")
            ps, pk = next_ps()
            for kc in range(8):
                mm(ps[:, 0:256], wb[:, kc, :], memnT[:, kc, :], kc == 0, kc == 7, [wk, "memnT"], [pk])
            evac(kmT[:, hh, :], ps[:, 0:256], [pk], ["kmT"])
            yield
        for hh in range(4):
            wb, wk = load_w(wkv_d[:, 512 + hh * 128:512 + (hh + 1) * 128], 8, 128, pool=poolC, eng="act")
            for mt in range(2):
                ps, pk = next_ps()
                for kc in range(8):
                    mm(ps[:, 0:128], memnT[:, kc, mt * 128:(mt + 1) * 128], wb[:, kc, :], kc == 0, kc == 7,
                       [wk, "memnT"], [pk])
                act(vm[:, mt, hh * 128:(hh + 1) * 128], ps[:, 0:128], AF.Copy, [pk], ["vm"], scale=0.5)
            yield
        for hh in range(4):
            wq, wqk = load_w(win_d[:, O_QM + hh * 128:O_QM + (hh + 1) * 128], 8, 128, pool=poolC, eng="act")
            wz, wzk = load_w(win_d[:, O_ZM + hh * 128:O_ZM + (hh + 1) * 128], 8, 128, pool=poolC, eng="act")
            for blk in range(4):
                bs = slice(blk * 512, (blk + 1) * 512)
                q_ = ci_ % 2
                ci_ += 1
                qmT, szm, pT, rden, tnum = qmT2[q_], szm2[q_], pT4[q_], rden2[q_], tnum2[q_]
                pso_, pso_k = (psE, "psE") if q_ == 0 else (psR, "psR0")
                psd_, psd_k = (psK, "psK") if q_ == 0 else (psTRf, "psTRb")
                ps, pk = next_ps()
                inproj(wq, wqk, range(8), blk, 128, ps, pk)
                act(qmT, ps[:, :], AF.Copy, [pk], [f"qmT{q_}"], scale=128.0 ** -0.5)
                ps, pk = next_ps()
                inproj(wz, wzk, range(8), blk, 128, ps, pk)
                silu2(szm, ps[:, :], pk, f"szm{q_}")
                yield
                for mt in range(2):
                    pst = psT[mt]
                    mm(pst[:, :], kmT[:, hh, mt * 128:(mt + 1) * 128], qmT, True, True, ["kmT", f"qmT{q_}"], [f"psT{mt}"])
                    act(pT[mt], pst[:, :], AF.Exp, [f"psT{mt}"], [f"pT{q_}{mt}"])
                for mt in range(2):
                    mm(pso_[:, :], vm[:, mt, hh * 128:(hh + 1) * 128], pT[mt], mt == 0, mt == 1,
                       ["vm", f"pT{q_}{mt}"], [pso_k])
                for mt in range(2):
                    mm(psd_[:, :], ones_bf[:, :], pT[mt], mt == 0, mt == 1, ["ones_bf", f"pT{q_}{mt}"], [psd_k])
                yield
                P.add("dve", (lambda o_, i_: (lambda e: e.reciprocal(out=o_, in_=i_)))(rden, psd_[:, :]), [psd_k], [f"rden{q_}"])
                tt("dve", tnum, pso_[:, :], rden, ALU.mult, [pso_k, f"rden{q_}"], [f"tnum{q_}"])
                tt("dve", ycT[:, hh, bs], tnum, szm, ALU.mult, [f"tnum{q_}", f"szm{q_}"], ["ycT"])
                yield
    N_C = 8 + 16 * 3

    par(phaseA(), N_A, phaseC(), N_C)
    dbg("yaT", yaT[:, :, :], [128, 4, T], "yaT")
    P.mark(4)
    dbg("ycT", ycT[:, :, :], [128, 4, T], "ycT")
    P.mark(5)

    P.barrier()
    cv = Carver()
    cv.off = 0
    yT = cv.get([128, 8, T], BF16)
    sg = [cv.get([128, 512], F32) for _ in range(2)]
    acc = cv.get([128, T], F32)
    wo = cv.get([128, 8, D], BF16)
    XT2 = [cv.get([128, D], F32) for _ in range(4)]
    junkO = cv.get([128, 512], BF16)
    tmo = [thb[0][:, :], thb[1][:, :]]
    tmpm = tmo[0]
    branches = ((yaT, "yaT", 4, wpp_d), (ybT, "ybT", 8, wpd_d), (ycT, "ycT", 4, wpm_d))
    gi_ = 0
    for oc in range(8):
        for br in range(3):
            ysrc, ykey, nk, wd = branches[br]
            wg, wgk = load_w(win_d[:, O_G + br * 1024 + oc * 128:O_G + br * 1024 + (oc + 1) * 128], 8, 128, eng="act")
            wp, wpk = load_w(wd[:, oc * 128:(oc + 1) * 128], nk, 128, eng="pool")
            if 8 <= oc * 3 + br < 16:
                cb = oc * 3 + br - 8
                P.add("pool", (lambda o_, i_: (lambda e: e.dma_start(out=o_, in_=i_)))(
                    wo[:, :, cb * 128:(cb + 1) * 128],
                    wo_d[:, cb * 128:(cb + 1) * 128].rearrange("(k p) n -> p k n", p=128)), [], ["wo"], gdma=True)
            for blk in range(4):
                bs = slice(blk * 512, (blk + 1) * 512)
                ps, pk = next_ps()
                inproj(wg, wgk, range(8), blk, 128, ps, pk)
                s_ = sg[gi_ % 2]
                sk_ = f"sg{gi_ % 2}"
                gi_ += 1
                act(s_, ps[:, :], AF.Sigmoid, [pk], [sk_])
                ps2, pk2 = next_ps()
                inproj(wp, wpk, range(nk), blk, 128, ps2, pk2, rhs_src=ysrc, rhs_key=ykey)
                if br == 0:
                    tt("dve", acc[:, bs], ps2[:, :], s_, ALU.mult, [pk2, sk_], ["acc"])
                elif br == 1:
                    tt("dve", tmpm, ps2[:, :], s_, ALU.mult, [pk2, sk_], ["thb0"])
                    tt("dve", acc[:, bs], acc[:, bs], tmpm, ALU.add, ["acc", "thb0"], ["acc"])
                else:
                    tt("dve", tmpm, ps2[:, :], s_, ALU.mult, [pk2, sk_], ["thb0"])
                    tt("dve", yT[:, oc, bs], acc[:, bs], tmpm, ALU.add, ["acc", "thb0"], ["yT"])
    dbg("yT", yT[:, :, :], [128, 8, T], "yT")
    P.mark(6)

    ssqo = small[:, 64:96]
    rso = small[:, 96:112]
    PRE = 3
    for i in range(PRE):
        dma(XT2[i % 4], x_d[i * 128:(i + 1) * 128, :], [], [f"XT2{i % 4}"])
    for i in range(NT):
        b = i % 4
        if i + PRE < NT:
            dma(XT2[(i + PRE) % 4], x_d[(i + PRE) * 128:(i + PRE + 1) * 128, :], [], [f"XT2{(i + PRE) % 4}"])
        pso = [[psT[0], psT[1]], [psE, psK], [ps01[0], ps01[1]], [psR, psTRf]][b]
        psok = [["psT0", "psT1"], ["psE", "psK"], ["ps0", "ps1"], ["psR0", "psTRb"]][b]
        for half in range(2):
            for kc in range(8):
                mm(pso[half][:, :], yT[:, kc, i * 128:(i + 1) * 128], wo[:, kc, half * 512:(half + 1) * 512],
                   kc == 0, kc == 7, ["yT", "wo"], [psok[half]])
            act(junkO, pso[half][:, :], AF.Square, [psok[half]], ["junkO", f"ssqo{i}"],
                accum_out=ssqo[:, 2 * i + half:2 * i + half + 1])
        tt("dve", rso[:, i:i + 1], ssqo[:, 2 * i:2 * i + 1], ssqo[:, 2 * i + 1:2 * i + 2], ALU.add,
           [f"ssqo{i}"], [f"rso{i}"])
        act(rso[:, i:i + 1], rso[:, i:i + 1], AF.Ln, [f"rso{i}"], [f"rso{i}"], scale=1.0 / D, bias=EPS)
        act(rso[:, i:i + 1], rso[:, i:i + 1], AF.Exp, [f"rso{i}"], [f"rso{i}"], scale=-0.5)
        for half in range(2):
            hs = slice(half * 512, (half + 1) * 512)
            tm_, tmk = [(sg[0], "sg0"), (sg[1], "sg1"), (tmo[0], "thb0"), (tmo[1], "thb1")][(2 * i + half) % 4]
            stt("dve", tm_, pso[half][:, :], rso[:, i:i + 1], pnw2_bc[:, hs],
                ALU.mult, ALU.mult, [psok[half], f"rso{i}", "pnw2_bc"], [tmk])
            tt("pool", XT2[b][:, hs], XT2[b][:, hs], tm_, ALU.add, [f"XT2{b}", tmk], [f"XT2{b}"])
        dma(out_d[i * 128:(i + 1) * 128, :], XT2[b], [f"XT2{b}"], ["out"])

    P.finalize()
    import contextlib
    with contextlib.ExitStack() as es:
        sems = {}
        for nm in ["pe", "act", "dve", "pool"] + [f"dma{i}" for i in range(NDMA)] + [f"gdma{i}" for i in range(NGDMA)]:
            sems[nm] = es.enter_context(nc.semaphore("s_" + nm))
        block = es.enter_context(nc.Block())

        @block.tensor
        def _(e):
            P.emit("pe", e, sems)

        @block.scalar
        def _(e):
            P.emit("act", e, sems)

        @block.vector
        def _(e):
            P.emit("dve", e, sems)

        @block.gpsimd
        def _(e):
            P.emit("pool", e, sems)

        @block.sync
        def _(e):
            P.emit("dma", e, sems, final_wait=True)
    stats = {k: len(v) for k, v in P.ops.items()}
    return nc, dbg_outs, stats


_CACHE = {}


def _prep_inputs(inputs):
    f = lambda a: np.ascontiguousarray(np.asarray(a, dtype=np.float32))
    shared = {
        "pre_norm_w": f(inputs["pre_norm_w"]).reshape(8, 128),
        "mem_norm_w": f(inputs["mem_norm_w"]).reshape(8, 128),
        "w_in": f(inputs["w_in"]).reshape(D, 9232),
        "conv_w": f(inputs["conv_w"]).reshape(96, 128),
        "a_log": f(inputs["a_log"]).reshape(8, 1),
        "dt_bias": f(inputs["dt_bias"]).reshape(8, 1),
        "dn_norm_w": f(inputs["dn_norm_w"]).reshape(1, 128),
        "pool_mix_w": f(inputs["pool_mix_w"]).reshape(512, 128),
        "pool_scale": f(inputs["pool_scale"]).reshape(4, 128),
        "w_mem_kv": f(inputs["w_mem_kv"]).reshape(D, 1024),
        "w_proj_pool": f(inputs["w_proj_pool"]).reshape(512, D),
        "w_proj_delta": f(inputs["w_proj_delta"]).reshape(D, D),
        "w_proj_mem": f(inputs["w_proj_mem"]).reshape(512, D),
        "w_out": f(inputs["w_out"]).reshape(D, D),
        "post_norm_w": f(inputs["post_norm_w"]).reshape(1, D),
    }
    x = f(inputs["x"])
    mem = f(inputs["mem"])
    in_maps = []
    for b in range(8):
        m = dict(shared)
        m["x"] = x[b]
        m["mem"] = mem[b]
        in_maps.append(m)
    return in_maps


def kernel(**inputs):
    if "nc" not in _CACHE:
        _CACHE["nc"] = build_nc()
    nc, dbg_outs, stats = _CACHE["nc"]
    in_maps = _prep_inputs(inputs)
    res = run_bass_kernel_spmd(nc, in_maps, core_ids=list(range(8)))
    out = np.stack([np.asarray(res.results[b]["out"], dtype=np.float32) for b in range(8)], axis=0)
    if DEBUG:
        _CACHE["dbg"] = {k: np.asarray(res.results[0][k]) for k in dbg_outs}
    return out
```
